# Optimizing a Trainium2 kernel written in Bass

```python
import math
import jax, jax.numpy as jnp
from jax import lax
import numpy as np

D_MODEL = 1024
BATCH = 4
SEQ = 8192
DEPTH = 2

MEM_LEN = 256
ATTN_HEADS = 16
ATTN_KV_HEADS = 4
HEAD_DIM = 64
WINDOW = 128
BLOCK = 128
SSD_EXPAND = 2
D_INNER = SSD_EXPAND * D_MODEL
SSD_HEAD_DIM = 64
SSD_HEADS = D_INNER // SSD_HEAD_DIM
SSD_GROUPS = 8
D_STATE = 128
CONV_K = 5
CHUNK = 128
X_HEADS = 4
X_HEAD_DIM = 128
D_FF = 4 * D_MODEL
EPS = 1e-6

Q_W = ATTN_HEADS * HEAD_DIM
KV_W = ATTN_KV_HEADS * HEAD_DIM
BC_W = SSD_GROUPS * D_STATE
XBC_W = D_INNER + 2 * BC_W
DT_W = 2 * SSD_HEADS
GATE_W = 2 * D_MODEL
OFF_K = Q_W
OFF_V = OFF_K + KV_W
OFF_Z = OFF_V + KV_W
OFF_XBC = OFF_Z + D_INNER
OFF_DT = OFF_XBC + XBC_W
OFF_GATE = OFF_DT + DT_W
D_IN_PROJ = OFF_GATE + GATE_W

kernel_name = "hybrid_gated_swa_ssd_encoder"


def rmsnorm(x, g):
    xf = x.astype(jnp.float32)
    y = xf * lax.rsqrt(jnp.mean(xf * xf, axis=-1, keepdims=True) + EPS)
    return (y * g.astype(jnp.float32)).astype(x.dtype)


def alibi_slopes(n_heads):
    return jnp.asarray(np.array([2.0 ** (-8.0 * (h + 1) / n_heads) for h in range(n_heads)], dtype=np.float32))


def windowed_gqa_sink(q, k, v, sink):
    b, l, H, d = q.shape
    kvh = k.shape[2]
    r = H // kvh
    nb = l // BLOCK
    qb = q.reshape(b, nb, BLOCK, kvh, r, d)
    pad = ((0, 0), (BLOCK, BLOCK), (0, 0), (0, 0))
    kp = jnp.pad(k, pad).reshape(b, nb + 2, BLOCK, kvh, d)
    vp = jnp.pad(v, pad).reshape(b, nb + 2, BLOCK, kvh, d)
    kw = jnp.concatenate([kp[:, :-2], kp[:, 1:-1], kp[:, 2:]], axis=2)
    vw = jnp.concatenate([vp[:, :-2], vp[:, 1:-1], vp[:, 2:]], axis=2)
    a_idx = np.arange(BLOCK)[:, None]
    w_idx = np.arange(3 * BLOCK)[None, :]
    rel = BLOCK + a_idx - w_idx
    in_band = np.abs(rel) <= WINDOW
    spos = (np.arange(nb)[:, None] - 1) * BLOCK + np.arange(3 * BLOCK)[None, :]
    valid = (spos >= 0) & (spos < l)
    mask = in_band[None] & valid[:, None, :]
    slopes = alibi_slopes(H).reshape(kvh, r, 1, 1)
    alibi = -slopes * jnp.asarray(np.abs(rel), jnp.float32)
    bias = jnp.where(jnp.asarray(mask)[:, None, None], alibi[None], -jnp.inf)
    s = jnp.einsum('bnqgrd,bnkgd->bngrqk', qb, kw).astype(jnp.float32) * (d ** -0.5) + bias[None]
    sink_f = sink.astype(jnp.float32).reshape(1, 1, kvh, r, 1, 1)
    m = jnp.maximum(jnp.max(s, axis=-1, keepdims=True), sink_f)
    p = jnp.exp(s - m)
    denom = jnp.sum(p, axis=-1, keepdims=True) + jnp.exp(sink_f - m)
    o = jnp.einsum('bngrqk,bnkgd->bnqgrd', (p / denom).astype(v.dtype), vw)
    return o.reshape(b, l, H * d)


def segsum_exp(a):
    T = a.shape[-1]
    cs = jnp.cumsum(a, axis=-1)
    diff = cs[..., :, None] - cs[..., None, :]
    tri = jnp.asarray(np.tril(np.ones((T, T), dtype=bool)))
    return jnp.where(tri, jnp.exp(jnp.where(tri, diff, 0.0)), 0.0)


def ssd_chunked(x, dt, A, Bm, Cm):
    b, l, H, P = x.shape
    G, N = Bm.shape[2], Bm.shape[3]
    r = H // G
    c = l // CHUNK
    xdt = (x * dt[..., None]).reshape(b, c, CHUNK, G, r, P)
    adt = jnp.moveaxis((dt * A).reshape(b, c, CHUNK, G, r), 2, -1)
    Bc = Bm.reshape(b, c, CHUNK, G, N)
    Cc = Cm.reshape(b, c, CHUNK, G, N)
    a_cs = jnp.cumsum(adt, axis=-1)
    CB = jnp.einsum('bclgn,bcsgn->bcgls', Cc, Bc)
    Wd = CB[:, :, :, None] * segsum_exp(adt)
    y_diag = jnp.einsum('bcgrls,bcsgrp->bclgrp', Wd, xdt)
    decay_states = jnp.exp(a_cs[..., -1:] - a_cs)
    states = jnp.einsum('bcsgn,bcgrs,bcsgrp->bcgrpn', Bc, decay_states, xdt)
    chunk_decay = jnp.exp(a_cs[..., -1])

    def step(carry, inp):
        st, dec = inp
        return carry * dec[..., None, None] + st, carry

    init = jnp.zeros((b, G, r, P, N), jnp.float32)
    _, prev = lax.scan(step, init, (jnp.moveaxis(states, 1, 0), jnp.moveaxis(chunk_decay, 1, 0)))
    prev = jnp.moveaxis(prev, 0, 1)
    y_off = jnp.einsum('bclgn,bcgrpn,bcgrl->bclgrp', Cc, prev, jnp.exp(a_cs))
    return (y_diag + y_off).reshape(b, l, H, P)


def depthwise_conv_centred(u, w, bias):
    out = lax.conv_general_dilated(u, w[:, None, :].astype(u.dtype), window_strides=(1,),
                                   padding=[(CONV_K // 2, CONV_K // 2)],
                                   dimension_numbers=('NWC', 'WIO', 'NWC'),
                                   feature_group_count=u.shape[-1])
    return out + bias.astype(u.dtype)


def ssd_branch(z, xbc, dt_raw, conv_w, conv_b, dt_bias, a_log, d_skip, norm_g):
    b, l, _ = z.shape
    f32 = jnp.float32
    xbc = jax.nn.silu(depthwise_conv_centred(xbc, conv_w, conv_b)).astype(f32)
    xs = xbc[..., :D_INNER].reshape(b, l, SSD_HEADS, SSD_HEAD_DIM)
    Bm = xbc[..., D_INNER:D_INNER + BC_W].reshape(b, l, SSD_GROUPS, D_STATE)
    Cm = xbc[..., D_INNER + BC_W:].reshape(b, l, SSD_GROUPS, D_STATE)
    A = -jnp.exp(a_log.astype(f32))
    dt = jax.nn.softplus(dt_raw.astype(f32).reshape(b, l, 2, SSD_HEADS) + dt_bias.astype(f32))
    y_fwd = ssd_chunked(xs, dt[:, :, 0], A[0], Bm, Cm)
    fl = lambda t: jnp.flip(t, axis=1)
    y_bwd = fl(ssd_chunked(fl(xs), fl(dt[:, :, 1]), A[1], fl(Bm), fl(Cm)))
    y = y_fwd + y_bwd + d_skip.astype(f32)[:, None] * xs
    y = y.reshape(b, l, D_INNER) * jax.nn.silu(z.astype(f32))
    yg = y.reshape(b, l, SSD_GROUPS, D_INNER // SSD_GROUPS)
    yg = yg * lax.rsqrt(jnp.mean(yg * yg, axis=-1, keepdims=True) + EPS)
    y = yg.reshape(b, l, D_INNER) * norm_g.astype(f32)
    return y.astype(z.dtype)


def cross_attention(h, mem_n, w_q, w_kv, w_o):
    b, l, _ = h.shape
    m = mem_n.shape[1]
    q = (h @ w_q).reshape(b, l, X_HEADS, X_HEAD_DIM)
    kv = mem_n @ w_kv
    k = kv[..., :X_HEADS * X_HEAD_DIM].reshape(b, m, X_HEADS, X_HEAD_DIM)
    v = kv[..., X_HEADS * X_HEAD_DIM:].reshape(b, m, X_HEADS, X_HEAD_DIM)
    s = jnp.einsum('blhd,bmhd->bhlm', q, k).astype(jnp.float32) * (X_HEAD_DIM ** -0.5)
    p = jax.nn.softmax(s, axis=-1)
    o = jnp.einsum('bhlm,bmhd->blhd', p.astype(v.dtype), v).reshape(b, l, X_HEADS * X_HEAD_DIM)
    return o @ w_o


def setup_inputs(seed: int = 0) -> dict:
    key = jax.random.key(seed)
    ks = iter(jax.random.split(key, 32))
    f32 = jnp.float32

    def dense(shape, fan_in):
        return jax.random.normal(next(ks), shape, f32) * (fan_in ** -0.5)

    def gain(shape):
        return 1.0 + 0.02 * jax.random.normal(next(ks), shape, f32)

    x = jax.random.normal(next(ks), (BATCH, SEQ, D_MODEL), f32)
    mem = jax.random.normal(next(ks), (BATCH, MEM_LEN, D_MODEL), f32)
    norm_mix = gain((DEPTH, D_MODEL))
    w_in = dense((DEPTH, D_MODEL, D_IN_PROJ), D_MODEL)
    attn_sink = 0.1 * jax.random.normal(next(ks), (DEPTH, ATTN_HEADS), f32)
    conv_w = dense((DEPTH, CONV_K, XBC_W), CONV_K)
    conv_b = 0.01 * jax.random.normal(next(ks), (DEPTH, XBC_W), f32)
    dt0 = jnp.exp(jax.random.uniform(next(ks), (DEPTH, 2, SSD_HEADS), f32,
                                     minval=math.log(1e-3), maxval=math.log(1e-1)))
    dt_bias = dt0 + jnp.log(-jnp.expm1(-dt0))
    a_log = jnp.log(jax.random.uniform(next(ks), (DEPTH, 2, SSD_HEADS), f32, minval=1.0, maxval=16.0))
    d_skip = gain((DEPTH, SSD_HEADS))
    ssd_norm = gain((DEPTH, D_INNER))
    w_attn_branch = dense((DEPTH, Q_W, D_MODEL), Q_W)
    w_ssd_branch = dense((DEPTH, D_INNER, D_MODEL), D_INNER)
    w_out = dense((DEPTH, D_MODEL, D_MODEL), D_MODEL)
    norm_cross = gain((DEPTH, D_MODEL))
    norm_mem = gain((DEPTH, D_MODEL))
    w_xq = dense((DEPTH, D_MODEL, X_HEADS * X_HEAD_DIM), D_MODEL)
    w_xkv = dense((DEPTH, D_MODEL, 2 * X_HEADS * X_HEAD_DIM), D_MODEL)
    w_xo = dense((DEPTH, X_HEADS * X_HEAD_DIM, D_MODEL), X_HEADS * X_HEAD_DIM)
    norm_ffn = gain((DEPTH, D_MODEL))
    w_up = dense((DEPTH, D_MODEL, D_FF), D_MODEL)
    w_down = dense((DEPTH, D_FF, D_MODEL), D_FF)
    norm_final = gain((D_MODEL,))
    return {"x": x, "mem": mem, "norm_mix": norm_mix, "w_in": w_in, "attn_sink": attn_sink,
            "conv_w": conv_w, "conv_b": conv_b, "dt_bias": dt_bias, "a_log": a_log,
            "d_skip": d_skip, "ssd_norm": ssd_norm, "w_attn_branch": w_attn_branch,
            "w_ssd_branch": w_ssd_branch, "w_out": w_out, "norm_cross": norm_cross,
            "norm_mem": norm_mem, "w_xq": w_xq, "w_xkv": w_xkv, "w_xo": w_xo,
            "norm_ffn": norm_ffn, "w_up": w_up, "w_down": w_down, "norm_final": norm_final}


def reference(x, mem, norm_mix, w_in, attn_sink, conv_w, conv_b, dt_bias, a_log, d_skip,
              ssd_norm, w_attn_branch, w_ssd_branch, w_out, norm_cross, norm_mem, w_xq,
              w_xkv, w_xo, norm_ffn, w_up, w_down, norm_final):
    b, l, _ = x.shape
    for i in range(DEPTH):
        h = rmsnorm(x, norm_mix[i])
        proj = h @ w_in[i]
        q = proj[..., :OFF_K].reshape(b, l, ATTN_HEADS, HEAD_DIM)
        k = proj[..., OFF_K:OFF_V].reshape(b, l, ATTN_KV_HEADS, HEAD_DIM)
        v = proj[..., OFF_V:OFF_Z].reshape(b, l, ATTN_KV_HEADS, HEAD_DIM)
        z = proj[..., OFF_Z:OFF_XBC]
        xbc = proj[..., OFF_XBC:OFF_DT]
        dt_raw = proj[..., OFF_DT:OFF_GATE]
        gate_a = proj[..., OFF_GATE:OFF_GATE + D_MODEL]
        gate_b = proj[..., OFF_GATE + D_MODEL:]
        attn = windowed_gqa_sink(q, k, v, attn_sink[i])
        ssd = ssd_branch(z, xbc, dt_raw, conv_w[i], conv_b[i], dt_bias[i], a_log[i],
                         d_skip[i], ssd_norm[i])
        merged = (jax.nn.sigmoid(gate_a) * (attn @ w_attn_branch[i])
                  + jax.nn.sigmoid(gate_b) * (ssd @ w_ssd_branch[i]))
        x = x + merged @ w_out[i]
        h = rmsnorm(x, norm_cross[i])
        x = x + cross_attention(h, rmsnorm(mem, norm_mem[i]), w_xq[i], w_xkv[i], w_xo[i])
        h = rmsnorm(x, norm_ffn[i])
        x = x + jnp.square(jax.nn.relu(h @ w_up[i])) @ w_down[i]
    return rmsnorm(x, norm_final)
```

```python
import types
import numpy as np
import ml_dtypes
import concourse.bass as bass
import concourse.mybir as mybir
from concourse.bass_utils import run_bass_kernel_spmd

F32 = mybir.dt.float32
BF16 = mybir.dt.bfloat16
U8 = mybir.dt.uint8
AF = mybir.ActivationFunctionType
ALU = mybir.AluOpType

D = 1024
T = 4096
HALO = 128
TE = T + HALO
NB = T // 128
EPS = 1e-6
NEG = -30000.0
ARENA = 200 * 1024

ENGS = ("pe", "act", "dve", "pool", "sp")
CARVE_LOG = []
ARENA_HW = [0]
LAST_PROG = [None]


def _freeze(fn):
    if fn is None or fn.__closure__ is None:
        return fn
    cells = []
    for c in fn.__closure__:
        try:
            cells.append(types.CellType(c.cell_contents))
        except ValueError:
            cells.append(c)
    return types.FunctionType(fn.__code__, fn.__globals__, fn.__name__, fn.__defaults__, tuple(cells))


class Tk:
    __slots__ = ("w", "r", "banks")

    def __init__(self, banks=()):
        self.w = None
        self.r = []
        self.banks = tuple(banks)


class Prog:
    def __init__(self, nc, n_dma_sems=8):
        self.nc = nc
        self.sems = {}
        self.cnt = {}
        self.cur = {}
        self.phase = 0
        for e in ("pe", "act", "dve", "pool"):
            self.sems[e] = nc.alloc_semaphore("s_" + e)
            self.cnt[e] = 0
            self.cur[e] = e
        self.dma_sems = {}
        for q in ("sp", "pool"):
            self.dma_sems[q] = []
            for i in range(n_dma_sems):
                k = "d_%s%d" % (q, i)
                self.sems[k] = nc.alloc_semaphore(k)
                self.cnt[k] = 0
                self.dma_sems[q].append(k)
        self.sems["cc"] = nc.alloc_semaphore("s_cc")
        self.cnt["cc"] = 0
        self.dma_rr = {"sp": 0, "pool": 0}
        self.streams = {e: [] for e in ENGS}
        self.waited = {e: {} for e in ENGS}
        self.marks = []
        self.armarks = []
        self.bank_last = [dict() for _ in range(8)]

    def _deps(self, eng, reads, writes):
        need = {}

        def add(tok):
            if tok is None:
                return
            k, v = tok
            if k.startswith("pe") and eng == "pe":
                return
            if need.get(k, 0) < v:
                need[k] = v
        for t in reads:
            add(t.w)
        for t in writes:
            add(t.w)
            for tok in t.r:
                add(tok)
        for t in list(reads) + list(writes):
            for b in t.banks:
                for e2, tok in self.bank_last[b].items():
                    if e2 != eng:
                        add(tok)
        out = []
        wd = self.waited[eng]
        for k, v in need.items():
            if wd.get(k, 0) < v:
                wd[k] = v
                out.append((k, v))
        return out

    def _commit(self, tok, reads, writes, eng=None):
        for t in list(reads) + list(writes):
            for b in t.banks:
                self.bank_last[b][eng] = tok
        for t in reads:
            t.r.append(tok)
            if len(t.r) > 64:
                mx = {}
                for k, v in t.r:
                    if mx.get(k, 0) < v:
                        mx[k] = v
                t.r = list(mx.items())
        for t in writes:
            t.w = tok
            t.r = []

    def op(self, eng, fn, reads=(), writes=()):
        waits = self._deps(eng, reads, writes)
        key = self.cur[eng]
        self.cnt[key] += 1
        tok = (key, self.cnt[key])
        self._commit(tok, reads, writes, eng)
        self.streams[eng].append((waits, _freeze(fn), (key, 1)))
        return tok

    def dma(self, q, out, in_, reads=(), writes=()):
        waits = self._deps(q, reads, writes)
        lst = self.dma_sems[q]
        k = lst[self.dma_rr[q] % len(lst)]
        self.dma_rr[q] += 1
        self.cnt[k] += 16
        tok = (k, self.cnt[k])
        self._commit(tok, reads, writes)

        def fn(e, out=out, in_=in_):
            return e.dma_start(out=out, in_=in_)
        self.streams[q].append((waits, fn, (k, 16)))
        return tok

    def cc(self, fn, reads=(), writes=()):
        waits = self._deps("pool", reads, writes)
        self.cnt["cc"] += 1
        tok = ("cc", self.cnt["cc"])
        self._commit(tok, reads, writes)
        self.streams["pool"].append((waits, _freeze(fn), ("cc", 1)))
        return tok

    def barrier(self):
        self.marks.append(dict(self.cnt))
        self.armarks.append(ARENA_HW[0]); ARENA_HW[0] = 0
        for e in ENGS:
            waits = []
            for k, v in self.cnt.items():
                if k.startswith("pe") and e == "pe":
                    continue
                if v > self.waited[e].get(k, 0):
                    self.waited[e][k] = v
                    waits.append((k, v))
            self.streams[e].append((waits, None, None))
        self.phase += 1
        for e in ("pe", "act", "dve", "pool"):
            key = "%s@%d" % (e, self.phase)
            self.sems[key] = self.nc.alloc_semaphore("s_%s_%d" % (e, self.phase))
            self.cnt[key] = 0
            self.cur[e] = key

    def emit(self):
        nc = self.nc
        prog = self
        with nc.Block() as block:
            def mk(ename):
                def body(e):
                    for waits, fn, inc in prog.streams[ename]:
                        for k, v in waits:
                            e.wait_ge(prog.sems[k], v)
                        if fn is not None:
                            fn(e).then_inc(prog.sems[inc[0]], inc[1])
                return body
            block.tensor(mk("pe"))
            block.scalar(mk("act"))
            block.vector(mk("dve"))
            block.gpsimd(mk("pool"))
            block.sync(mk("sp"))


class Arena:
    def __init__(self, nc):
        self.t = nc.alloc_sbuf_tensor("arena", [128, ARENA], U8)
        self.off = 0

    def reset(self, to=0):
        self.off = to

    def carve(self, shape, dt):
        esz = 4 if dt == F32 else 2
        n = int(np.prod(shape[1:])) * esz
        assert self.off + n <= ARENA, ("arena overflow", self.off, n)
        v = self.t[:, self.off:self.off + n].bitcast(dt)
        CARVE_LOG.append((self.off, tuple(shape), str(dt)))
        self.off += (n + 63) // 64 * 64
        ARENA_HW[0] = max(ARENA_HW[0], self.off)
        if len(shape) == 3:
            v = v.rearrange("p (a b) -> p a b", a=shape[1])
        elif len(shape) == 4:
            v = v.rearrange("p (a b c) -> p a b c", a=shape[1], b=shape[2])
        return v


class Ring:
    def __init__(self, ar, n, shape, dt):
        self.aps = [ar.carve(shape, dt) for _ in range(n)]
        self.tk = [Tk() for _ in range(n)]
        self.n = n
        self.i = -1

    def next(self):
        self.i += 1
        j = self.i % self.n
        return self.aps[j], self.tk[j]


class K:
    pass


def _psum_banks(nc):
    pst = nc.alloc_psum_tensor("psum_all", [128, 4096], F32)
    banks = [pst[:, i * 512:(i + 1) * 512] for i in range(8)]
    return pst, banks


def rmsnorm_tile(k, xin, txin, n, g_ap, out_fn, sq, tsq, rst, trst, psb, tps, out_dt_eng=("dve", "pool")):
    p = k.p
    p.op("act", lambda e: e.activation(sq[:, :, 0:n], xin[:, :, 0:n], AF.Square), reads=[txin], writes=[tsq])
    for c in range(8):
        p.op("pe", lambda e, c=c: e.matmul(psb[:, 0:n], k.ones32, sq[:, c, 0:n], start=(c == 0), stop=(c == 7)),
             reads=[tsq, k.tconst], writes=[tps])
    p.op("act", lambda e: e.activation(rst[:, 0:n], psb[:, 0:n], AF.Ln, bias=k.epsc[:, 0:1], scale=1.0 / D), reads=[tps, k.tconst], writes=[trst])
    p.op("act", lambda e: e.activation(rst[:, 0:n], rst[:, 0:n], AF.Exp, scale=-0.5), reads=[trst], writes=[trst])
    for c in range(8):
        o, ot = out_fn(c)
        p.op("dve", lambda e, c=c, o=o: e.scalar_tensor_tensor(o, xin[:, c, 0:n], g_ap[:, c:c + 1], rst[:, 0:n], ALU.mult, ALU.mult),
             reads=[txin, trst, k.tconst], writes=ot)


def build(ncores=8, nlayers=2, dbg=(), stop_after=None, bvar=9):
    nc = bass.Bass("TRN2", target_bir_lowering=False)
    k = K()
    k.nc = nc
    p = k.p = Prog(nc)
    ar = k.ar = Arena(nc)
    PSALL, PS = _psum_banks(nc)
    k.ps = PS
    k.tps = [Tk(banks=(i,)) for i in range(8)]

    def din(name, shape, dt=F32):
        return nc.dram_tensor(name, list(shape), dt, kind="ExternalInput").ap()

    def dscr(name, shape, dt):
        kind = "ExternalOutput" if name in dbg else "Internal"
        return nc.dram_tensor(name, list(shape), dt, kind=kind).ap()

    xT_in = din("xT", [D, TE])
    memT_in = din("memT", [D, 256])
    sel_in = din("sel", [128, 2])
    cst_in = din("cst", [128, 1152])
    em_in = din("em", [128, 2 * 3 * 16 * 128])
    gn_in = din("gn", [128, 8 * 9])
    L = []
    for l in range(nlayers):
        d = {}
        d["wfm"] = din("wfm%d" % l, [D, 7424])
        d["wtm"] = din("wtm%d" % l, [D, 2304])
        d["wdt"] = din("wdt%d" % l, [D, 64])
        d["cw"] = din("cw%d" % l, [128, 32 * 5])
        d["cb"] = din("cb%d" % l, [128, 32])
        d["cbrow"] = din("cbrow%d" % l, [1, 4096])
        d["dtb"] = din("dtb%d" % l, [1, 64])
        d["alog"] = din("alog%d" % l, [1, 64])
        d["dsk"] = din("dsk%d" % l, [1, 32])
        d["sng"] = din("sng%d" % l, [1, 2048])
        d["sink"] = din("sink%d" % l, [1, 16])
        d["wa"] = din("wa%d" % l, [1024, 1024])
        d["ws"] = din("ws%d" % l, [2048, 1024])
        d["wo"] = din("wo%d" % l, [1024, 1024])
        d["wxq"] = din("wxq%d" % l, [1024, 512])
        d["wxkv"] = din("wxkv%d" % l, [1024, 1024])
        d["wxo"] = din("wxo%d" % l, [512, 1024])
        d["wup"] = din("wup%d" % l, [1024, 4096])
        d["wdn"] = din("wdn%d" % l, [4096, 1024])
        L.append(d)
    outT = nc.dram_tensor("outT", [D, T], F32, kind="ExternalOutput").ap()

    xT = dscr("xTs", [D, TE], F32)
    qT = dscr("qT", [64, 16, T], BF16)
    kdT = dscr("kdT", [64, 4, TE], BF16)
    gT = dscr("gT", [2048, T], BF16)
    uT = dscr("uT", [4096, TE + 4], BF16)
    vv = dscr("vv", [TE, 256], BF16)
    zz = dscr("zz", [T, 2048], BF16)
    dtr = dscr("dtr", [T, 64], F32)
    attnT = dscr("attnT", [64, 16, T], BF16)
    ssdT = dscr("ssdT", [2048, T], BF16)
    ypart = dscr("ypart", [T, 2048], F32)
    cTs = dscr("cTs", [1024, T], BF16)
    bsv = dscr("bsv", [T, 1024], BF16)
    xdsb = dscr("xdsb", [T, 2048], BF16)
    ecsv = dscr("ecsv", [T, 64], F32)
    cdcv = dscr("cdcv", [NB, 128, 64], F32)
    ccS_in = nc.dram_tensor("ccS_in", [128, 2048], F32).ap()
    ccS_out = nc.dram_tensor("ccS_out", [256, 2048], F32).ap()
    ccX_in = nc.dram_tensor("ccX_in", [D, HALO], F32).ap()
    ccX_out = nc.dram_tensor("ccX_out", [2 * D, HALO], F32).ap()
    k.tdram = Tk()
    pairs = [[2 * i, 2 * i + 1] for i in range(ncores // 2)]

    cbf = ar.carve([128, 1024], BF16)
    c32 = ar.carve([128, 640], F32)
    gn = ar.carve([128, 72], F32)
    sel = ar.carve([128, 2], F32)
    mhalf = ar.carve([128, 1], F32)
    epsc = ar.carve([128, 1], F32)
    k.epsc = epsc
    k.tconst = Tk()
    k.ident = cbf[:, 0:128]
    k.triU = cbf[:, 128:256]
    k.triL = cbf[:, 256:384]
    k.ntriU = cbf[:, 384:512]
    k.ntriL = cbf[:, 512:640]
    k.maskA = cbf[:, 640:768]
    k.maskB = cbf[:, 768:896]
    k.onesb = cbf[:, 896:1024]
    k.triU32 = c32[:, 0:128]
    k.triL32 = c32[:, 128:256]
    k.ones32 = c32[:, 256:384]
    k.ident32 = c32[:, 384:512]
    k.J32 = c32[:, 512:640]
    k.mhalf = mhalf
    k.gn = gn
    k.sel = sel
    p.dma("pool", cbf, cst_in[:, 0:1024], writes=[k.tconst])
    p.dma("sp", c32[:, 384:512], cst_in[:, 0:128], writes=[k.tconst])
    p.dma("sp", c32[:, 512:640], cst_in[:, 1024:1152], writes=[k.tconst])
    p.dma("sp", c32[:, 0:256], cst_in[:, 128:384], writes=[k.tconst])
    p.dma("sp", c32[:, 256:384], cst_in[:, 896:1024], writes=[k.tconst])
    p.dma("sp", gn, gn_in, writes=[k.tconst])
    p.dma("sp", sel, sel_in, writes=[k.tconst])
    p.op("pool", lambda e: e.memset(mhalf, -0.5), writes=[k.tconst])
    p.op("pool", lambda e: e.memset(epsc, EPS), writes=[k.tconst])
    p.dma("sp", xT, xT_in)
    base_off = ar.off
    p.barrier()

    xT_v = xT.rearrange("(c p) t -> p c t", p=128)

    def phase_A(l):
        W = L[l]
        ar.reset(base_off)
        hT = ar.carve([128, 8, TE], BF16)
        thT = [Tk() for _ in range(9)]
        xin = Ring(ar, 2, [128, 8, 512], F32)
        sq = Ring(ar, 1, [128, 8, 512], F32)
        rst = Ring(ar, 2, [128, 512], F32)
        wfm = Ring(ar, 2, [128, 8, 512], BF16)
        ost = Ring(ar, 2, [128, TE], BF16)
        ostm = Ring(ar, 3, [128, 512], F32)
        zero = ar.carve([128, 2], BF16)
        tz = Tk()
        g_ap = k.gn[:, (4 * l) * 8:(4 * l) * 8 + 8]
        tiles = [(i * 512, 512) for i in range(8)] + [(T, HALO)]
        p.op("pool", lambda e: e.memset(zero, 0.0), writes=[tz])
        for ch in range(32):
            pass
        for q_ in range(4):
            p.dma("sp", uT.rearrange("(c p) t -> p c t", p=128)[:, q_ * 8:(q_ + 1) * 8, 0:2], zero.unsqueeze(1).to_broadcast([128, 8, 2]), reads=[tz])
        for ti, (t0, n) in enumerate(tiles):
            xa, xt = xin.next()
            p.dma("sp", xa[:, :, 0:n], xT_v[:, :, t0:t0 + n], writes=[xt])
            sa, st = sq.next()
            ra, rt = rst.next()
            bi = ti % 2
            rmsnorm_tile(k, xa, xt, n, g_ap, lambda c, t0=t0, n=n, ti=ti: (hT[:, c, t0:t0 + n], [thT[ti]]),
                         sa, st, ra, rt, PS[bi], k.tps[bi])
        pb = 0
        ev = 0
        for cb4 in range(15):
            wa_, wt = wfm.next()
            nbk = 4 if cb4 < 14 else 2
            p.dma("pool", wa_[:, :, 0:nbk * 128], W["wfm"].rearrange("(c p) n -> p c n", p=128)[:, :, cb4 * 512:cb4 * 512 + nbk * 128], writes=[wt])
            for j in range(nbk):
                blk = cb4 * 4 + j
                if blk < 8:
                    kind, dst, ntl = "q", None, 8
                elif blk < 10:
                    kind, dst, ntl = "k", None, 9
                elif blk < 26:
                    kind, dst, ntl = "g", gT[(blk - 10) * 128:(blk - 9) * 128, :], 8
                else:
                    kind, dst, ntl = "u", uT[(blk - 26) * 128:(blk - 25) * 128, 2:2 + TE], 9
                oa, ot = ost.next()
                for ti in range(ntl):
                    t0, n = tiles[ti]
                    bank = 2 + (pb % 4)
                    pb += 1
                    for c in range(8):
                        p.op("pe", lambda e, c=c, bank=bank, t0=t0, n=n, wa_=wa_, j=j:
                             e.matmul(PS[bank][:, 0:n], wa_[:, c, j * 128:(j + 1) * 128], hT[:, c, t0:t0 + n], start=(c == 0), stop=(c == 7)),
                             reads=[wt, thT[ti]], writes=[k.tps[bank]])
                    if kind == "g":
                        p.op("act", lambda e, bank=bank, t0=t0, n=n, oa=oa: e.activation(oa[:, t0:t0 + n], PS[bank][:, 0:n], AF.Sigmoid),
                             reads=[k.tps[bank]], writes=[ot])
                    else:
                        eng = "act" if (ev % 2 == 0) else "dve"
                        ev += 1
                        if eng == "act":
                            p.op("act", lambda e, bank=bank, t0=t0, n=n, oa=oa: e.activation(oa[:, t0:t0 + n], PS[bank][:, 0:n], AF.Copy),
                                 reads=[k.tps[bank]], writes=[ot])
                        else:
                            p.op("dve", lambda e, bank=bank, t0=t0, n=n, oa=oa: e.tensor_copy(oa[:, t0:t0 + n], PS[bank][:, 0:n]),
                                 reads=[k.tps[bank]], writes=[ot])
                ntok = T if ntl == 8 else TE
                if kind == "q":
                    p.dma("sp", qT[:, 2 * blk, :], oa[0:64, 0:T], reads=[ot])
                    p.dma("sp", qT[:, 2 * blk + 1, :], oa[64:128, 0:T], reads=[ot])
                elif kind == "k":
                    p.dma("sp", kdT[:, 2 * (blk - 8), :], oa[0:64, 0:TE], reads=[ot])
                    p.dma("sp", kdT[:, 2 * (blk - 8) + 1, :], oa[64:128, 0:TE], reads=[ot])
                else:
                    p.dma("sp", dst[:, 0:ntok], oa[:, 0:ntok], reads=[ot])
        groups = [("v", W["wtm"][:, 0:256], 256, 33)] + \
                 [("z%d" % i, W["wtm"][:, 256 + i * 512:256 + (i + 1) * 512], 512, 32) for i in range(4)] + \
                 [("dt", W["wdt"], 64, 32)]
        for gi, (name, wsrc, ncol, nblk) in enumerate(groups):
            wa_, wt = wfm.next()
            p.dma("pool", wa_[:, :, 0:ncol], wsrc.rearrange("(c p) n -> p c n", p=128), writes=[wt])
            for tb in range(nblk):
                bank = 2 + (pb % 4)
                pb += 1
                ti = min(tb // 4, 8)
                for c in range(8):
                    p.op("pe", lambda e, c=c, bank=bank, tb=tb, wa_=wa_, ncol=ncol:
                         e.matmul(PS[bank][:, 0:ncol], hT[:, c, tb * 128:(tb + 1) * 128], wa_[:, c, 0:ncol], start=(c == 0), stop=(c == 7)),
                         reads=[wt, thT[ti]], writes=[k.tps[bank]])
                oa, ot = ostm.next()
                if name == "dt":
                    ov = oa[:, 0:64]
                    dst = dtr[tb * 128:(tb + 1) * 128, :]
                elif name == "v":
                    ov = oa.bitcast(BF16)[:, 0:256]
                    dst = vv[tb * 128:(tb + 1) * 128, :]
                else:
                    zi = int(name[1:])
                    ov = oa.bitcast(BF16)[:, 0:512]
                    dst = zz[tb * 128:(tb + 1) * 128, zi * 512:(zi + 1) * 512]
                eng = "act" if (ev % 2 == 0) else "dve"
                ev += 1
                if eng == "act":
                    p.op("act", lambda e, bank=bank, ov=ov, ncol=ncol: e.activation(ov, PS[bank][:, 0:ncol], AF.Copy),
                         reads=[k.tps[bank]], writes=[ot])
                else:
                    p.op("dve", lambda e, bank=bank, ov=ov, ncol=ncol: e.tensor_copy(ov, PS[bank][:, 0:ncol]),
                         reads=[k.tps[bank]], writes=[ot])
                p.dma("sp", dst, ov, reads=[ot])
        p.barrier()

    def bcast_row(ap2d, n):
        return ap2d.rearrange("a n -> (a n)").partition_broadcast(128)

    def pipelined(n, stages, skew=1):
        for ts_ in range(n + (len(stages) - 1) * skew):
            for si, f in enumerate(stages):
                i_ = ts_ - si * skew
                if 0 <= i_ < n:
                    f(i_)

    def phase_B(l):
        W = L[l]
        ar.reset(base_off)
        kd_sb = ar.carve([128, 4, TE], BF16)
        vones = ar.carve([128, 33, 4, 128], BF16)
        emb = ar.carve([128, 2, 6144], BF16)
        esink = ar.carve([128, 16], F32)
        qr = Ring(ar, 3, [128, 16, 128], BF16)
        pt = Ring(ar, 3, [128, 3, 512], BF16)
        dn = Ring(ar, 2, [128, 512], F32)
        ob = Ring(ar, 2, [128, 16, 128], BF16)
        tk_, tv_, te_, ts_ = Tk(), Tk(), Tk(), Tk()
        p.dma("sp", kd_sb[0:64], kdT, writes=[tk_])
        for g in range(4):
            for b3 in range(3):
                p.dma("sp", vones[:, b3 * 11:(b3 + 1) * 11, g, 0:64], vv.rearrange("(b p) c -> p b c", p=128)[:, b3 * 11:(b3 + 1) * 11, g * 64:(g + 1) * 64], writes=[tv_])
        p.op("pool", lambda e: e.memset(vones[:, :, :, 64:128], 1.0), writes=[tv_])
        for hl in range(2):
            p.dma("pool", emb[:, hl, :], em_in[:, hl * 6144:(hl + 1) * 6144], writes=[te_])
        p.dma("sp", esink, bcast_row(W["sink"], 16), writes=[ts_])
        p.op("act", lambda e: e.activation(esink, esink, AF.Exp), reads=[ts_], writes=[ts_])
        ctx = {}

        def stA(it):
            i, g = it // 4, it % 4
            if g == 0:
                qa, qt = qr.next()
                p.dma("sp", qa[0:64], qT[:, :, i * 128:(i + 1) * 128], writes=[qt])
                ctx[("q", i)] = (qa, qt)
                ctx[("o", i)] = ob.next()
            qa, qt = ctx[("q", i)]
            jl = [jj for jj in range(3) if 0 <= i - 1 + jj <= 32]
            jlo, jhi = jl[0], jl[-1] + 1
            sset = (it % 2) * 3
            for jj in jl:
                j = i - 1 + jj
                S3 = PS[sset + jj].rearrange("p (r q) -> p r q", r=4)
                for hl in range(2):
                    p.op("pe", lambda e: e.matmul(S3, k.ident, emb[:, hl, jj * 2048 + g * 512:jj * 2048 + (g + 1) * 512].rearrange("p (r q) -> p r q", r=4), start=(hl == 0), stop=False),
                         reads=[te_, k.tconst], writes=[k.tps[sset + jj]])
                for r in range(4):
                    h = 4 * g + r
                    p.op("pe", lambda e: e.matmul(PS[sset + jj][:, r * 128:(r + 1) * 128], kd_sb[0:64, g, j * 128:(j + 1) * 128],
                                                  qa[0:64, h, :], start=False, stop=(r == 3)),
                         reads=[tk_, qt], writes=[k.tps[sset + jj]])
            pa, ptk = pt.next()
            s3 = PSALL[:, sset * 512:(sset + 3) * 512].rearrange("p (j n) -> p j n", j=3)
            p.op("act", lambda e: e.activation(pa[:, jlo:jhi, :], s3[:, jlo:jhi, :], AF.Exp, scale=0.125),
                 reads=[k.tps[sset + jj] for jj in jl], writes=[ptk])
            ctx[("p", it)] = (pa, ptk, jl)

        def stB(it):
            i, g = it // 4, it % 4
            pa, ptk, jl = ctx.pop(("p", it))
            oa, ot = ctx[("o", i)]
            pvb = 6 + (it % 2)
            for n_, jj in enumerate(jl):
                j = i - 1 + jj
                p.op("pe", lambda e: e.matmul(PS[pvb][:, :], vones[:, j, g, :], pa[:, jj, :], start=(n_ == 0), stop=(n_ == len(jl) - 1)),
                     reads=[tv_, ptk], writes=[k.tps[pvb]])
            da, dt_ = dn.next()
            p.op("dve", lambda e: e.tensor_tensor(da[64:128, :].rearrange("p (r q) -> p r q", r=4), PS[pvb][64:128, :].rearrange("p (r q) -> p r q", r=4),
                                                  esink[64:128, 4 * g:4 * g + 4].unsqueeze(2).to_broadcast([64, 4, 128]), ALU.add),
                 reads=[k.tps[pvb], ts_], writes=[dt_])
            p.op("act", lambda e: e.activation(da[64:128, :], da[64:128, :], AF.Ln), reads=[dt_], writes=[dt_])
            p.op("act", lambda e: e.activation(da[64:128, :], da[64:128, :], AF.Exp, scale=-1.0), reads=[dt_], writes=[dt_])
            p.op("dve", lambda e: e.tensor_tensor(oa[0:64, 4 * g:4 * g + 4, :], PS[pvb][0:64, :].rearrange("p (r q) -> p r q", r=4),
                                                  da[64:128, :].rearrange("p (r q) -> p r q", r=4), ALU.mult),
                 reads=[k.tps[pvb], dt_], writes=[ot])
            if g == 3:
                p.dma("sp", attnT[:, :, i * 128:(i + 1) * 128], oa[0:64, :, :], reads=[ot])

        pipelined(NB * 4, [stA, stB])
        p.barrier()

    def phase_C1(l):
        W = L[l]
        ar.reset(base_off)
        dgs = ar.carve([128, 32, 5, 128], BF16)
        cw = ar.carve([128, 32, 5], F32)
        cbrow = ar.carve([128, 4096], BF16)
        dtb = ar.carve([128, 64], F32)
        arep = ar.carve([128, 64], F32)
        dsk = ar.carve([128, 32], F32)
        SA = ar.carve([128, 2048], F32)
        SAbf = [ar.carve([128, 2048], BF16) for _ in range(3)]
        tSA = [Tk() for _ in range(8)]
        tSAbf = [[Tk() for _ in range(8)] for _ in range(3)]
        uar = Ring(ar, 2, [128, 32, 132], BF16)
        dar = Ring(ar, 2, [128, 64], F32)
        xsr = Ring(ar, 3, [128, 3072], BF16)
        bcr = Ring(ar, 3, [128, 16, 128], BF16)
        smr = Ring(ar, 2, [128, 10, 64], F32)
        ahr = Ring(ar, 2, [128, 2, 64], BF16)
        xdr = Ring(ar, 2, [128, 5, 2048], BF16)
        er = Ring(ar, 2, [128, 512], F32)
        wdr = Ring(ar, 4, [128, 512], BF16)
        tmr = Ring(ar, 2, [128, 256], F32)
        ypr = Ring(ar, 2, [128, 2048], F32)
        tc_ = Tk()
        p.dma("sp", cw.rearrange("p a b -> p (a b)"), W["cw"], writes=[tc_])
        p.dma("pool", cbrow[0:1, :], W["cbrow"], writes=[tc_])
        p.dma("sp", dtb, bcast_row(W["dtb"], 64), writes=[tc_])
        p.dma("sp", arep, bcast_row(W["alog"], 64), writes=[tc_])
        p.dma("sp", dsk, bcast_row(W["dsk"], 32), writes=[tc_])
        p.op("act", lambda e: e.activation(arep, arep, AF.Exp), reads=[tc_], writes=[tc_])
        p.op("dve", lambda e: e.tensor_scalar(arep, arep, -1.0, None, ALU.mult), reads=[tc_], writes=[tc_])
        p.op("pool", lambda e: e.memset(SA, 0.0), writes=tSA)
        p.op("pool", lambda e: e.memset(SAbf[0], 0.0), writes=tSAbf[0])
        tdg = Tk()
        for ch in range(32):
            for kk in range(5):
                p.op("dve", lambda e, ch=ch, kk=kk: e.tensor_scalar(dgs[:, ch, kk, :], k.ident, cw[:, ch, kk:kk + 1], None, ALU.mult),
                     reads=[tc_, k.tconst], writes=[tdg])
        uT_v = uT.rearrange("(c p) t -> p c t", p=128)
        tconv = [k.tps[0], k.tps[1]]
        tcs = k.tps[2]
        tcbt = k.tps[3]
        tD = [k.tps[4], k.tps[5]]
        tY = k.tps[6]
        tYo = k.tps[7]
        tst = k.tps[7]
        cbr = Ring(ar, 2, [128, 128], F32)
        cvi = 0
        ones_row = k.onesb[0:1, 0:128]
        ctx = {}
        cvi_ = [0]

        def stA1(c):
            cvi = cvi_[0]
            ua, ut = uar.next()
            for q_ in range(4):
                p.dma("sp", ua[:, q_ * 8:(q_ + 1) * 8, :], uT_v[:, q_ * 8:(q_ + 1) * 8, c * 128:c * 128 + 132], writes=[ut])
            da, dat = dar.next()
            p.dma("sp", da, dtr[c * 128:(c + 1) * 128, :], writes=[dat])
            xs, xst = xsr.next()
            bc, bct = bcr.next()
            for q4 in range(6):
                bank = cvi % 2
                cvi += 1
                for j in range(4):
                    ch = q4 * 4 + j
                    o_ = PS[bank][:, j * 128:(j + 1) * 128]
                    for kk in range(5):
                        p.op("pe", lambda e, o_=o_, ua=ua, ch=ch, kk=kk: e.matmul(o_, ua[:, ch, kk:kk + 128], dgs[:, ch, kk, :], start=(kk == 0), stop=False),
                             reads=[ut, tdg], writes=[tconv[bank]])
                    p.op("pe", lambda e, o_=o_, ch=ch: e.matmul(o_, ones_row, cbrow[0:1, ch * 128:(ch + 1) * 128], start=False, stop=True),
                         reads=[tc_, k.tconst], writes=[tconv[bank]])
                p.op("act", lambda e, bank=bank, xs=xs, q4=q4: e.activation(xs[:, q4 * 512:(q4 + 1) * 512], PS[bank], AF.Silu),
                     reads=[tconv[bank]], writes=[xst])
            for q4 in range(4):
                bank = cvi % 2
                cvi += 1
                for j in range(4):
                    ch = 16 + q4 * 4 + j
                    o_ = PS[bank][:, j * 128:(j + 1) * 128]
                    for kk in range(5):
                        p.op("pe", lambda e, o_=o_, ua=ua, ch=ch, kk=kk: e.matmul(o_, dgs[:, ch, kk, :], ua[:, ch, kk:kk + 128], start=(kk == 0), stop=False),
                             reads=[ut, tdg], writes=[tconv[bank]])
                    p.op("pe", lambda e, o_=o_, ch=ch: e.matmul(o_, cbrow[0:1, ch * 128:(ch + 1) * 128], ones_row, start=False, stop=True),
                         reads=[tc_, k.tconst], writes=[tconv[bank]])
                p.op("act", lambda e, bank=bank, bc=bc, q4=q4: e.activation(bc[:, q4 * 4:(q4 + 1) * 4, :].rearrange("p a b -> p (a b)"), PS[bank], AF.Silu),
                     reads=[tconv[bank]], writes=[bct])
            BT = bc[:, 0:8, :]
            CT = bc[:, 8:16, :]
            Btm = xs[:, 2048:3072]
            cvi_[0] = cvi
            ctx[("a1", c)] = (da, dat, xs, xst, bc, bct, BT, CT, Btm)

        def stA2(c):
            da, dat, xs, xst, bc, bct, BT, CT, Btm = ctx.pop(("a1", c))
            sm, smt = smr.next()
            ah, aht = ahr.next()
            t1, dtv, a32, cs, ecs, cdec, dd, dstt, wv = [sm[:, i_, :] for i_ in range(9)]
            ahi, alo = ah[:, 0, :], ah[:, 1, :]
            p.op("dve", lambda e, t1=t1, da=da: e.tensor_tensor(t1, da, dtb, ALU.add), reads=[dat, tc_], writes=[smt])
            p.op("act", lambda e, t1=t1: e.activation(t1, t1, AF.Exp), reads=[smt], writes=[smt])
            p.op("act", lambda e, t1=t1, dtv=dtv: e.activation(dtv, t1, AF.Ln, bias=1.0), reads=[smt], writes=[smt])
            p.op("dve", lambda e, a32=a32, dtv=dtv: e.tensor_tensor(a32, dtv, arep, ALU.mult), reads=[smt, tc_], writes=[smt])
            p.op("dve", lambda e, ahi=ahi, a32=a32: e.tensor_copy(ahi, a32), reads=[smt], writes=[aht])
            p.op("dve", lambda e, alo=alo, a32=a32, ahi=ahi: e.tensor_tensor(alo, a32, ahi, ALU.subtract), reads=[smt, aht], writes=[aht])
            p.op("pe", lambda e, a32=a32: e.matmul(PS[2][:, 0:32], k.triU32, a32[:, 0:32], start=True, stop=True), reads=[smt, k.tconst], writes=[tcs])
            p.op("pe", lambda e, a32=a32: e.matmul(PS[2][:, 32:64], k.triL32, a32[:, 32:64], start=True, stop=True), reads=[smt, k.tconst], writes=[tcs])
            p.op("pe", lambda e, a32=a32: e.matmul(PS[2][:, 64:128], k.ones32, a32[:, 0:64], start=True, stop=True), reads=[smt, k.tconst], writes=[tcs])
            p.op("act", lambda e, cs=cs: e.activation(cs, PS[2][:, 0:64], AF.Copy), reads=[tcs], writes=[smt])
            p.op("act", lambda e, ecs=ecs: e.activation(ecs, PS[2][:, 0:64], AF.Exp), reads=[tcs], writes=[smt])
            p.op("act", lambda e, cdec=cdec: e.activation(cdec, PS[2][:, 64:128], AF.Exp), reads=[tcs], writes=[smt])
            p.op("dve", lambda e, dd=dd, cs=cs: e.tensor_tensor(dd, PS[2][:, 64:128], cs, ALU.subtract), reads=[tcs, smt], writes=[smt])
            p.op("act", lambda e, dd=dd, dstt=dstt: e.activation(dstt, dd, AF.Exp), reads=[smt], writes=[smt])
            p.op("dve", lambda e, wv=wv, dtv=dtv, dstt=dstt: e.tensor_tensor(wv, dtv, dstt, ALU.mult), reads=[smt], writes=[smt])
            xd, xdt_ = xdr.next()
            xs3 = xs[:, 0:2048].rearrange("p (h q) -> p h q", h=32)

            def bcm(eng, idx, src32, reads):
                p.op(eng, lambda e, idx=idx, src32=src32: e.tensor_tensor(xd[:, idx, :].rearrange("p (h q) -> p h q", h=32), xs3,
                                                                          src32.unsqueeze(2).to_broadcast([128, 32, 64]), ALU.mult),
                     reads=[xst] + reads, writes=[xdt_])
            bcm("dve", 0, dtv[:, 0:32], [smt])
            bcm("pool", 1, dtv[:, 32:64], [smt])
            bcm("dve", 2, wv[:, 0:32], [smt])
            bcm("pool", 3, wv[:, 32:64], [smt])
            bcm("pool", 4, dsk, [tc_])
            ctx[c] = (xs, xst, bc, bct, BT, CT, Btm, ecs, cdec, smt, ahi, alo, aht, xd, xdt_)

        def stB(c):
            xs, xst, bc, bct, BT, CT, Btm, ecs, cdec, smt, ahi, alo, aht, xd, xdt_ = ctx.pop(c)
            yp, ypt = ypr.next()
            cur, nxt = c % 3, (c + 1) % 3
            wdsd = {}
            for g in range(8):
                stp = PS[7][:, 256:512]
                p.op("pe", lambda e: e.matmul(stp, Btm[:, g * 128:(g + 1) * 128], xd[:, 2, g * 256:(g + 1) * 256], start=True, stop=True),
                     reads=[xst, xdt_], writes=[tst])
                SAg = SA[:, g * 256:(g + 1) * 256]
                p.op("pool", lambda e: e.tensor_tensor(SAg.rearrange("p (r q) -> p r q", r=4), SAg.rearrange("p (r q) -> p r q", r=4),
                                                       cdec[:, 4 * g:4 * g + 4].unsqueeze(2).to_broadcast([128, 4, 64]), ALU.mult),
                     reads=[smt, tSA[g]], writes=[tSA[g]])
                p.op("dve", lambda e: e.tensor_tensor(SAg, stp, SAg, ALU.add), reads=[tst, tSA[g]], writes=[tSA[g]])
                p.op("act", lambda e: e.activation(SAbf[nxt][:, g * 256:(g + 1) * 256], SAg, AF.Copy),
                     reads=[tSA[g]], writes=[tSAbf[nxt][g]])

            def S1(g):
                cbt_ps = PS[3][:, 0:128]
                p.op("pe", lambda e, cbt_ps=cbt_ps, BT=BT, CT=CT, g=g: e.matmul(cbt_ps, BT[:, g, :], CT[:, g, :], start=True, stop=True),
                     reads=[bct], writes=[tcbt])
                cbs, cbst = cbr.next()
                p.op("act", lambda e: e.activation(cbs, cbt_ps, AF.Copy), reads=[tcbt], writes=[cbst])
                wds = []
                for di in range(2):
                    base = 32 * di + 4 * g
                    tri = k.triU if di == 0 else k.triL
                    ntri = k.ntriU if di == 0 else k.ntriL
                    msk = k.maskA if di == 0 else k.maskB
                    Db = PS[4 + di]
                    Db3 = Db.rearrange("p (r l) -> p r l", r=4)
                    for hl, src in enumerate((ahi, alo)):
                        p.op("pe", lambda e, Db3=Db3, src=src, base=base, ntri=ntri, hl=hl:
                             e.matmul(Db3, ntri, src[:, base:base + 4].unsqueeze(2).to_broadcast([128, 4, 128]), start=(hl == 0), stop=False),
                             reads=[aht, k.tconst], writes=[tD[di]])
                    for r in range(4):
                        for hl, src in enumerate((ahi, alo)):
                            p.op("pe", lambda e, Db=Db, r=r, src=src, base=base, tri=tri, hl=hl:
                                 e.matmul(Db[:, r * 128:(r + 1) * 128], src[:, base + r:base + r + 1].to_broadcast([128, 128]), tri, start=False, stop=False),
                                 reads=[aht, k.tconst], writes=[tD[di]])
                    p.op("pe", lambda e, Db3=Db3, msk=msk: e.matmul(Db3, k.ident, msk.unsqueeze(1).to_broadcast([128, 4, 128]), start=False, stop=True),
                         reads=[k.tconst], writes=[tD[di]])
                    ea, eat = er.next()
                    p.op("act", lambda e, ea=ea, Db=Db: e.activation(ea, Db, AF.Exp), reads=[tD[di]], writes=[eat])
                    wd, wdt_ = wdr.next()
                    p.op("dve" if di == 0 else "pool", lambda e, wd=wd, ea=ea, cbs=cbs:
                         e.tensor_tensor(wd.rearrange("p (r l) -> p r l", r=4), ea.rearrange("p (r l) -> p r l", r=4),
                                         cbs.unsqueeze(1).to_broadcast([128, 4, 128]), ALU.mult),
                         reads=[eat, cbst], writes=[wdt_])
                    wds.append((wd, wdt_))
                wdsd[g] = wds

            def S2(g):
                wds = wdsd.pop(g)
                Yb = PS[6][:, 0:256]
                Yo = PS[7][:, 0:256]
                p.op("pe", lambda e, Yb=Yb, xd=xd, g=g: e.matmul(Yb, k.ident, xd[:, 4, g * 256:(g + 1) * 256], start=True, stop=False),
                     reads=[xdt_, k.tconst], writes=[tY])
                for di in range(2):
                    wd, wdt_ = wds[di]
                    for r in range(4):
                        hh = 4 * g + r
                        p.op("pe", lambda e, Yb=Yb, wd=wd, xd=xd, di=di, r=r, hh=hh:
                             e.matmul(Yb[:, r * 64:(r + 1) * 64], wd[:, r * 128:(r + 1) * 128], xd[:, di, hh * 64:(hh + 1) * 64],
                                      start=False, stop=(di == 1 and r == 3)),
                             reads=[wdt_, xdt_], writes=[tY])
                p.op("pe", lambda e, Yo=Yo, CT=CT, g=g, cur=cur: e.matmul(Yo, CT[:, g, :], SAbf[cur][:, g * 256:(g + 1) * 256], start=True, stop=True),
                     reads=[bct, tSAbf[cur][g]], writes=[tYo])
                tm, tmt = tmr.next()
                p.op("dve", lambda e, tm=tm, Yo=Yo, ecs=ecs, g=g:
                     e.tensor_tensor(tm.rearrange("p (r q) -> p r q", r=4), Yo.rearrange("p (r q) -> p r q", r=4),
                                     ecs[:, 4 * g:4 * g + 4].unsqueeze(2).to_broadcast([128, 4, 64]), ALU.mult),
                     reads=[tYo, smt], writes=[tmt])
                p.op("dve", lambda e, yp=yp, Yb=Yb, tm=tm, g=g: e.tensor_tensor(yp[:, g * 256:(g + 1) * 256], Yb, tm, ALU.add),
                     reads=[tY, tmt], writes=[ypt])
            S1(0)
            for g in range(8):
                if g + 1 < 8:
                    S1(g + 1)
                S2(g)
            p.dma("sp", ypart[c * 128:(c + 1) * 128, :], yp, reads=[ypt])
            p.dma("sp", cTs.rearrange("(g p) t -> p g t", p=128)[:, :, c * 128:(c + 1) * 128], CT, reads=[bct])
            p.dma("sp", bsv[c * 128:(c + 1) * 128, :], Btm, reads=[xst])
            p.dma("sp", xdsb[c * 128:(c + 1) * 128, :], xd[:, 3, :], reads=[xdt_])
            p.dma("sp", ecsv[c * 128:(c + 1) * 128, :], ecs, reads=[smt])
            p.dma("sp", cdcv[c], cdec, reads=[smt])
        pipelined(NB, [stA1, stA2, stB])
        p.dma("sp", ccS_in, SA, reads=tSA)
        p.barrier()

    def phase_C2(l):
        W = L[l]
        ar.reset(base_off)
        SB = ar.carve([128, 2048], F32)
        SBbf = [ar.carve([128, 2048], BF16) for _ in range(3)]
        G = ar.carve([128, 2, 2048], F32)
        ngr = ar.carve([128, 2048], F32)
        tSB = [Tk() for _ in range(8)]
        tSBbf = [[Tk() for _ in range(8)] for _ in range(3)]
        tG, tng, tcc = Tk(), Tk(), Tk()
        ypr = Ring(ar, 3, [128, 2048], F32)
        xdr = Ring(ar, 2, [128, 2048], BF16)
        btr = Ring(ar, 2, [128, 1024], BF16)
        ctr = Ring(ar, 3, [128, 8, 128], BF16)
        ecr = Ring(ar, 3, [128, 2, 64], F32)
        ztr = Ring(ar, 3, [128, 2048], BF16)
        szr = Ring(ar, 1, [128, 2048], F32)
        yor = Ring(ar, 2, [128, 2048], BF16)
        str_ = Ring(ar, 2, [128, 16, 128], BF16)
        tmr = Ring(ar, 2, [128, 256], F32)
        ssr = Ring(ar, 2, [128, 16], F32)
        jk = ar.carve([128, 256], F32)
        tjk = Tk()
        p.cc(lambda e: e.collective_compute("AllGather", ALU.bypass, replica_groups=pairs, ins=[ccS_in.opt()], outs=[ccS_out.opt()]),
             writes=[tcc])
        p.dma("sp", G, ccS_out.rearrange("(r p) f -> p r f", p=128), reads=[tcc], writes=[tG])
        p.dma("sp", ngr, bcast_row(W["sng"], 2048), writes=[tng])
        p.op("dve", lambda e: e.tensor_scalar(SB, G[:, 0, :], k.sel[:, 0:1], None, ALU.mult), reads=[tG, k.tconst], writes=tSB)
        p.op("dve", lambda e: e.scalar_tensor_tensor(SB, G[:, 1, :], k.sel[:, 1:2], SB, ALU.mult, ALU.add), reads=[tG, k.tconst] + tSB, writes=tSB)
        p.op("act", lambda e: e.activation(SBbf[0], SB, AF.Copy), reads=tSB, writes=tSBbf[0])
        tYo = [k.tps[0], k.tps[4]]
        tst = [k.tps[1], k.tps[5]]
        ttr = [k.tps[2], k.tps[3]]
        PSB = PSALL.bitcast(BF16)
        ctx = {}
        itc = [0]

        def stA(ci):
            c = NB - 1 - ci
            xdb, xdbt = xdr.next()
            p.dma("sp", xdb, xdsb[c * 128:(c + 1) * 128, :], writes=[xdbt])
            bt_, btt = btr.next()
            p.dma("sp", bt_, bsv[c * 128:(c + 1) * 128, :], writes=[btt])
            ec, ect = ecr.next()
            p.dma("sp", ec[:, 0, :], ecsv[c * 128:(c + 1) * 128, :], writes=[ect])
            p.dma("sp", ec[:, 1, :], cdcv[c], writes=[ect])
            ct_, ctt = ctr.next()
            p.dma("sp", ct_, cTs.rearrange("(g p) t -> p g t", p=128)[:, :, c * 128:(c + 1) * 128], writes=[ctt])
            yp, ypt = ypr.next()
            p.dma("sp", yp, ypart[c * 128:(c + 1) * 128, :], writes=[ypt])
            zt, ztt = ztr.next()
            p.dma("sp", zt, zz[c * 128:(c + 1) * 128, :], writes=[ztt])
            nxt = (ci + 1) % 3
            for g in range(8):
                hb = g % 2
                stp = PS[1 if hb == 0 else 5][:, 0:256]
                p.op("pe", lambda e: e.matmul(stp, bt_[:, g * 128:(g + 1) * 128], xdb[:, g * 256:(g + 1) * 256], start=True, stop=True),
                     reads=[btt, xdbt], writes=[tst[hb]])
                SBg = SB[:, g * 256:(g + 1) * 256]
                p.op("pool", lambda e: e.tensor_tensor(SBg.rearrange("p (r q) -> p r q", r=4), SBg.rearrange("p (r q) -> p r q", r=4),
                                                       ec[:, 1, 32 + 4 * g:32 + 4 * g + 4].unsqueeze(2).to_broadcast([128, 4, 64]), ALU.mult),
                     reads=[ect, tSB[g]], writes=[tSB[g]])
                p.op("dve", lambda e: e.tensor_tensor(SBg, stp, SBg, ALU.add), reads=[tst[hb], tSB[g]], writes=[tSB[g]])
                p.op("act", lambda e: e.activation(SBbf[nxt][:, g * 256:(g + 1) * 256], SBg, AF.Copy),
                     reads=[tSB[g]], writes=[tSBbf[nxt][g]])
            ctx[ci] = (c, yp, ypt, zt, ztt, ct_, ctt, ec, ect)

        def stB1(ci):
            c, yp, ypt, zt, ztt, ct_, ctt, ec, ect = ctx.pop(ci)
            cur = ci % 3
            sz, szt = szr.next()
            p.op("act", lambda e: e.activation(sz, zt, AF.Silu), reads=[ztt], writes=[szt])
            for g in range(8):
                hb = g % 2
                Yo = PS[0 if hb == 0 else 4][:, 0:256]
                p.op("pe", lambda e: e.matmul(Yo, ct_[:, g, :], SBbf[cur][:, g * 256:(g + 1) * 256], start=True, stop=True),
                     reads=[ctt, tSBbf[cur][g]], writes=[tYo[hb]])
                tm, tmt = tmr.next()
                p.op("dve", lambda e: e.tensor_tensor(tm.rearrange("p (r q) -> p r q", r=4), Yo.rearrange("p (r q) -> p r q", r=4),
                                                      ec[:, 0, 32 + 4 * g:32 + 4 * g + 4].unsqueeze(2).to_broadcast([128, 4, 64]), ALU.mult),
                     reads=[tYo[hb], ect], writes=[tmt])
                ypg = yp[:, g * 256:(g + 1) * 256]
                p.op("dve", lambda e: e.tensor_tensor(ypg, ypg, tm, ALU.add), reads=[tmt, ypt], writes=[ypt])
            p.op("dve", lambda e: e.tensor_tensor(yp, yp, sz, ALU.mult), reads=[ypt, szt], writes=[ypt])
            ctx[("b1", ci)] = (c, yp, ypt)

        def stB2(ci):
            c, yp, ypt = ctx.pop(("b1", ci))
            ss, sst = ssr.next()
            for g in range(8):
                p.op("act", lambda e: e.activation(jk, yp[:, g * 256:(g + 1) * 256], AF.Square, accum_out=ss[:, g:g + 1]),
                     reads=[ypt], writes=[tjk, sst])
            p.op("act", lambda e: e.activation(ss[:, 8:16], ss[:, 0:8], AF.Ln, bias=k.epsc[:, 0:1], scale=1.0 / 256), reads=[sst, k.tconst], writes=[sst])
            p.op("act", lambda e: e.activation(ss[:, 8:16], ss[:, 8:16], AF.Exp, scale=-0.5), reads=[sst], writes=[sst])
            yo, yot = yor.next()
            for g in range(8):
                p.op("dve", lambda e: e.scalar_tensor_tensor(yo[:, g * 256:(g + 1) * 256], yp[:, g * 256:(g + 1) * 256], ss[:, 8 + g:9 + g],
                                                             ngr[:, g * 256:(g + 1) * 256], ALU.mult, ALU.mult),
                     reads=[ypt, sst, tng], writes=[yot])
            ctx[("b2", ci)] = (c, yo, yot)

        def stB3(ci):
            c, yo, yot = ctx.pop(("b2", ci))
            sT, sTt = str_.next()
            for q4 in range(4):
                tb = itc[0] % 2
                itc[0] += 1
                bank = 2 + tb
                pst = PSB[:, bank * 1024:bank * 1024 + 512]
                for j in range(4):
                    kk = q4 * 4 + j
                    p.op("pe", lambda e: e.transpose(pst[:, j * 128:(j + 1) * 128], yo[:, kk * 128:(kk + 1) * 128], k.ident),
                         reads=[yot, k.tconst], writes=[ttr[tb]])
                p.op("act", lambda e: e.activation(sT[:, q4 * 4:(q4 + 1) * 4, :].rearrange("p a b -> p (a b)"), pst, AF.Copy),
                     reads=[ttr[tb]], writes=[sTt])
            for q_ in range(2):
                p.dma("sp", ssdT.rearrange("(kk p) t -> p kk t", p=128)[:, q_ * 8:(q_ + 1) * 8, c * 128:(c + 1) * 128], sT[:, q_ * 8:(q_ + 1) * 8, :], reads=[sTt])

        pipelined(NB, [stA, stB1, stB2, stB3])
        p.barrier()

    def phase_D(l):
        W = L[l]
        ar.reset(base_off)
        NT = 256
        wa = ar.carve([128, 16, 1024], BF16)
        ws = ar.carve([128, 16, 1024], BF16)
        wo = ar.carve([128, 8, 1024], BF16)
        tw = Tk()
        twa, tws, two = Tk(), Tk(), Tk()
        wa_src = W["wa"].rearrange("(h d) n -> d h n", d=64)
        for h4 in range(4):
            p.dma("pool", wa[0:64, h4 * 4:(h4 + 1) * 4, :], wa_src[:, h4 * 4:(h4 + 1) * 4, :], writes=[twa])
        ws_src = W["ws"].rearrange("(kk p) n -> p kk n", p=128)
        for h4 in range(4):
            p.dma("pool", ws[:, h4 * 4:(h4 + 1) * 4, :], ws_src[:, h4 * 4:(h4 + 1) * 4, :], writes=[tws])
        wo_src = W["wo"].rearrange("(kk p) n -> p kk n", p=128)
        for h4 in range(2):
            p.dma("pool", wo[:, h4 * 4:(h4 + 1) * 4, :], wo_src[:, h4 * 4:(h4 + 1) * 4, :], writes=[two])
        atr = Ring(ar, 2, [128, 16, NT], BF16)
        sr = Ring(ar, 2, [128, 16, NT], BF16)
        gtr = Ring(ar, 2, [128, 16, NT], BF16)
        xtr = Ring(ar, 2, [128, 8, NT], F32)
        mgr = Ring(ar, 2, [128, 8, NT], BF16)
        t1r = Ring(ar, 2, [128, NT], F32)
        t2r = Ring(ar, 2, [128, NT], F32)
        ssd_v = ssdT.rearrange("(kk p) t -> p kk t", p=128)
        g_v = gT.rearrange("(kk p) t -> p kk t", p=128)
        tP = k.tps
        n = 0
        for t in range(T // NT):
            t0 = t * NT
            at, att = atr.next()
            p.dma("sp", at[0:64], attnT[:, :, t0:t0 + NT], writes=[att])
            st, stt = sr.next()
            p.dma("sp", st, ssd_v[:, :, t0:t0 + NT], writes=[stt])
            gt, gtt = gtr.next()
            p.dma("sp", gt, g_v[:, :, t0:t0 + NT], writes=[gtt])
            xt, xtt = xtr.next()
            p.dma("sp", xt, xT_v[:, :, t0:t0 + NT], writes=[xtt])
            mg, mgt = mgr.next()
            for oc in range(8):
                ba, bs = (n % 2), 2 + (n % 2)
                n += 1
                for h in range(16):
                    p.op("pe", lambda e, ba=ba, h=h, oc=oc, at=at: e.matmul(PS[ba][:, 0:NT], wa[0:64, h, oc * 128:(oc + 1) * 128], at[0:64, h, :], start=(h == 0), stop=(h == 15)),
                         reads=[twa, att], writes=[tP[ba]])
                for kk in range(16):
                    p.op("pe", lambda e, bs=bs, kk=kk, oc=oc, st=st: e.matmul(PS[bs][:, 0:NT], ws[:, kk, oc * 128:(oc + 1) * 128], st[:, kk, :], start=(kk == 0), stop=(kk == 15)),
                         reads=[tws, stt], writes=[tP[bs]])
                t1, t1t = t1r.next()
                t2, t2t = t2r.next()
                p.op("dve", lambda e, t1=t1, ba=ba, gt=gt, oc=oc: e.tensor_tensor(t1, PS[ba][:, 0:NT], gt[:, oc, :], ALU.mult), reads=[tP[ba], gtt], writes=[t1t])
                p.op("dve", lambda e, t2=t2, bs=bs, gt=gt, oc=oc: e.tensor_tensor(t2, PS[bs][:, 0:NT], gt[:, 8 + oc, :], ALU.mult), reads=[tP[bs], gtt], writes=[t2t])
                p.op("pool", lambda e, mg=mg, oc=oc, t1=t1, t2=t2: e.tensor_tensor(mg[:, oc, :], t1, t2, ALU.add), reads=[t1t, t2t], writes=[mgt])
            for oc in range(8):
                bo = 4 + (oc % 2)
                for kk in range(8):
                    p.op("pe", lambda e, bo=bo, kk=kk, oc=oc, mg=mg: e.matmul(PS[bo][:, 0:NT], wo[:, kk, oc * 128:(oc + 1) * 128], mg[:, kk, :], start=(kk == 0), stop=(kk == 7)),
                         reads=[two, mgt], writes=[tP[bo]])
                p.op("dve", lambda e, bo=bo, xt=xt, oc=oc: e.tensor_tensor(xt[:, oc, :], PS[bo][:, 0:NT], xt[:, oc, :], ALU.add), reads=[tP[bo], xtt], writes=[xtt])
            p.dma("sp", xT_v[:, :, t0:t0 + NT], xt, reads=[xtt])
        p.barrier()

    def phase_E(l):
        W = L[l]
        ar.reset(base_off)
        NT = 512
        g_x = k.gn[:, (4 * l + 1) * 8:(4 * l + 1) * 8 + 8]
        g_m = k.gn[:, (4 * l + 2) * 8:(4 * l + 2) * 8 + 8]
        mt = ar.carve([128, 8, 256], F32)
        memn = ar.carve([128, 8, 256], BF16)
        wk = ar.carve([128, 8, 512], BF16)
        wv = ar.carve([128, 8, 512], BF16)
        wq = ar.carve([128, 8, 512], BF16)
        wxo = ar.carve([128, 4, 1024], BF16)
        kxT = ar.carve([128, 4, 256], BF16)
        vx = ar.carve([128, 2, 512], BF16)
        sq = ar.carve([128, 8, NT], F32)
        rst = ar.carve([128, NT], F32)
        tw, tmt_, tmn, tsq, trs, tkx, tvx = [Tk() for _ in range(7)]
        xtr = Ring(ar, 2, [128, 8, NT], F32)
        htr = Ring(ar, 1, [128, 8, NT], BF16)
        qxr = Ring(ar, 1, [128, 4, NT], BF16)
        ptr = Ring(ar, 2, [128, 2, NT], BF16)
        rcr = Ring(ar, 2, [128, NT], F32)
        oxr = Ring(ar, 1, [128, 4, NT], BF16)
        tP = k.tps
        kv_src = W["wxkv"].rearrange("(c p) n -> p c n", p=128)
        p.dma("sp", mt, memT_in.rearrange("(c p) m -> p c m", p=128), writes=[tmt_])
        p.dma("pool", wk, kv_src[:, :, 0:512], writes=[tw])
        p.dma("pool", wv, kv_src[:, :, 512:1024], writes=[tw])
        p.dma("pool", wq, W["wxq"].rearrange("(c p) n -> p c n", p=128), writes=[tw])
        p.dma("pool", wxo, W["wxo"].rearrange("(h p) n -> p h n", p=128), writes=[tw])
        rmsnorm_tile(k, mt, tmt_, 256, g_m, lambda c: (memn[:, c, :], [tmn]), sq, tsq, rst, trs, PS[0], tP[0])
        for h in range(4):
            b = 1 + (h % 2)
            for c in range(8):
                p.op("pe", lambda e, b=b, c=c, h=h: e.matmul(PS[b][:, 0:256], wk[:, c, h * 128:(h + 1) * 128], memn[:, c, :], start=(c == 0), stop=(c == 7)),
                     reads=[tw, tmn], writes=[tP[b]])
            p.op("act", lambda e, b=b, h=h: e.activation(kxT[:, h, :], PS[b][:, 0:256], AF.Copy), reads=[tP[b]], writes=[tkx])
        for mb in range(2):
            b = 3 + mb
            for c in range(8):
                p.op("pe", lambda e, b=b, c=c, mb=mb: e.matmul(PS[b], memn[:, c, mb * 128:(mb + 1) * 128], wv[:, c, :], start=(c == 0), stop=(c == 7)),
                     reads=[tw, tmn], writes=[tP[b]])
            p.op("dve", lambda e, b=b, mb=mb: e.tensor_copy(vx[:, mb, :], PS[b]), reads=[tP[b]], writes=[tvx])
        n = 0
        for t in range(T // NT):
            t0 = t * NT
            xt, xtt = xtr.next()
            p.dma("sp", xt, xT_v[:, :, t0:t0 + NT], writes=[xtt])
            ht, htt = htr.next()
            rmsnorm_tile(k, xt, xtt, NT, g_x, lambda c, ht=ht, htt=htt: (ht[:, c, :], [htt]), sq, tsq, rst, trs, PS[0], tP[0])
            qx, qxt = qxr.next()
            for h in range(4):
                b = 1 + (h % 2)
                for c in range(8):
                    p.op("pe", lambda e, b=b, c=c, h=h, ht=ht: e.matmul(PS[b], wq[:, c, h * 128:(h + 1) * 128], ht[:, c, :], start=(c == 0), stop=(c == 7)),
                         reads=[tw, htt], writes=[tP[b]])
                if h % 2 == 0:
                    p.op("act", lambda e, b=b, h=h, qx=qx: e.activation(qx[:, h, :], PS[b], AF.Copy), reads=[tP[b]], writes=[qxt])
                else:
                    p.op("dve", lambda e, b=b, h=h, qx=qx: e.tensor_copy(qx[:, h, :], PS[b]), reads=[tP[b]], writes=[qxt])
            ox, oxt = oxr.next()
            for h in range(4):
                pt_, ptt = ptr.next()
                for mb in range(2):
                    b = 3 + mb
                    p.op("pe", lambda e, b=b, h=h, mb=mb, qx=qx: e.matmul(PS[b], kxT[:, h, mb * 128:(mb + 1) * 128], qx[:, h, :], start=True, stop=True),
                         reads=[tkx, qxt], writes=[tP[b]])
                    p.op("act", lambda e, b=b, mb=mb, pt_=pt_: e.activation(pt_[:, mb, :], PS[b], AF.Exp, scale=float(128 ** -0.5)), reads=[tP[b]], writes=[ptt])
                for mb in range(2):
                    p.op("pe", lambda e, h=h, mb=mb, pt_=pt_: e.matmul(PS[5], vx[:, mb, h * 128:(h + 1) * 128], pt_[:, mb, :], start=(mb == 0), stop=(mb == 1)),
                         reads=[tvx, ptt], writes=[tP[5]])
                for mb in range(2):
                    p.op("pe", lambda e, mb=mb, pt_=pt_: e.matmul(PS[6], k.onesb, pt_[:, mb, :], start=(mb == 0), stop=(mb == 1)),
                         reads=[k.tconst, ptt], writes=[tP[6]])
                rc, rct = rcr.next()
                p.op("act", lambda e, rc=rc: e.activation(rc, PS[6], AF.Ln), reads=[tP[6]], writes=[rct])
                p.op("act", lambda e, rc=rc: e.activation(rc, rc, AF.Exp, scale=-1.0), reads=[rct], writes=[rct])
                p.op("dve", lambda e, rc=rc, ox=ox, h=h: e.tensor_tensor(ox[:, h, :], PS[5], rc, ALU.mult), reads=[tP[5], rct], writes=[oxt])
            for oc in range(8):
                b = 7 if oc % 2 == 0 else 1
                for h in range(4):
                    p.op("pe", lambda e, b=b, h=h, oc=oc, ox=ox: e.matmul(PS[b], wxo[:, h, oc * 128:(oc + 1) * 128], ox[:, h, :], start=(h == 0), stop=(h == 3)),
                         reads=[tw, oxt], writes=[tP[b]])
                p.op("dve", lambda e, b=b, xt=xt, oc=oc: e.tensor_tensor(xt[:, oc, :], PS[b], xt[:, oc, :], ALU.add), reads=[tP[b], xtt], writes=[xtt])
            p.dma("sp", xT_v[:, :, t0:t0 + NT], xt, reads=[xtt])
        p.barrier()

    def phase_F(l, last):
        W = L[l]
        ar.reset(base_off)
        NT = 256
        g_f = k.gn[:, (4 * l + 3) * 8:(4 * l + 3) * 8 + 8]
        g_o = k.gn[:, 64:72]
        wup = ar.carve([128, 8, 4096], BF16)
        wdn = ar.carve([128, 32, 1024], BF16)
        tw = Tk()
        up_src = W["wup"].rearrange("(c p) n -> p c n", p=128)
        dn_src = W["wdn"].rearrange("(c p) n -> p c n", p=128)
        twu = [Tk() for _ in range(8)]
        twd = [Tk() for _ in range(8)]
        for i_ in range(8):
            p.dma("pool", wup[:, :, i_ * 512:(i_ + 1) * 512], up_src[:, :, i_ * 512:(i_ + 1) * 512], writes=[twu[i_]])
        for i_ in range(8):
            p.dma("pool", wdn[:, i_ * 4:(i_ + 1) * 4, :], dn_src[:, i_ * 4:(i_ + 1) * 4, :], writes=[twd[i_]])
        sq = ar.carve([128, 8, NT], F32)
        rst = ar.carve([128, NT], F32)
        tsq, trs = Tk(), Tk()
        xtr = Ring(ar, 2, [128, 8, NT], F32)
        htr = Ring(ar, 2, [128, 8, NT], BF16)
        acr = Ring(ar, 1, [128, 32, NT], BF16)
        rlr = Ring(ar, 2, [128, NT], F32)
        fo = ar.carve([128, 8, NT], F32)
        tfo = Tk()
        tP = k.tps
        out_v = outT.rearrange("(c p) t -> p c t", p=128)
        ctx = {}

        def stN(t):
            t0 = t * NT
            xt, xtt = xtr.next()
            p.dma("sp", xt, xT_v[:, :, t0:t0 + NT], writes=[xtt])
            ht, htt = htr.next()
            rmsnorm_tile(k, xt, xtt, NT, g_f, lambda c, ht=ht, htt=htt: (ht[:, c, :], [htt]), sq, tsq, rst, trs, PS[0], tP[0])
            ctx[t] = (xt, xtt, ht, htt)

        def stU(t):
            xt, xtt, ht, htt = ctx[t]
            ac, act_ = acr.next()
            for fc in range(32):
                b = 1 + (fc % 3)
                for c in range(8):
                    p.op("pe", lambda e: e.matmul(PS[b][:, 0:NT], wup[:, c, fc * 128:(fc + 1) * 128], ht[:, c, :], start=(c == 0), stop=(c == 7)),
                         reads=[twu[fc // 4], htt], writes=[tP[b]])
                rl, rlt = rlr.next()
                p.op("act", lambda e: e.activation(rl, PS[b][:, 0:NT], AF.Relu), reads=[tP[b]], writes=[rlt])
                p.op("pool", lambda e: e.tensor_tensor(ac[:, fc, :], rl, rl, ALU.mult), reads=[rlt], writes=[act_])
            ctx[t] = (xt, xtt, ac, act_)

        def stD(t):
            t0 = t * NT
            xt, xtt, ac, act_ = ctx.pop(t)
            for oc in range(8):
                b = 4 + (oc % 3)
                for fc in range(32):
                    p.op("pe", lambda e: e.matmul(PS[b][:, 0:NT], wdn[:, fc, oc * 128:(oc + 1) * 128], ac[:, fc, :], start=(fc == 0), stop=(fc == 31)),
                         reads=[twd[fc // 4], act_], writes=[tP[b]])
                p.op("dve", lambda e: e.tensor_tensor(xt[:, oc, :], PS[b][:, 0:NT], xt[:, oc, :], ALU.add), reads=[tP[b], xtt], writes=[xtt])
            if last:
                rmsnorm_tile(k, xt, xtt, NT, g_o, lambda c: (fo[:, c, :], [tfo]), sq, tsq, rst, trs, PS[7], tP[7])
                p.dma("sp", out_v[:, :, t0:t0 + NT], fo, reads=[tfo])
            else:
                p.dma("sp", xT_v[:, :, t0:t0 + NT], xt, reads=[xtt])

        ntile = T // NT
        stN(0)
        for t in range(ntile):
            stU(t)
            if t + 1 < ntile:
                stN(t + 1)
            stD(t)
        p.barrier()

    def phase_X2():
        ar.reset(base_off)
        xs = ar.carve([128, 8, 128], F32)
        x1 = ar.carve([128, 128], F32)
        xr = ar.carve([128, 8, 128], F32)
        G = ar.carve([128, 2, 8, 128], F32)
        xh = ar.carve([128, 8, 128], F32)
        txs, tx1, txr, tG, txh, tcc = [Tk() for _ in range(6)]
        tP = k.tps
        p.dma("sp", xs, xT_v[:, :, T - 128:T], writes=[txs])
        for c in range(8):
            p.op("pe", lambda e, c=c: e.matmul(PS[0][:, 0:128], xs[:, c, :], k.ident32, start=True, stop=True), reads=[txs, k.tconst], writes=[tP[0]])
            p.op("act", lambda e: e.activation(x1, PS[0][:, 0:128], AF.Copy), reads=[tP[0]], writes=[tx1])
            p.op("pe", lambda e: e.matmul(PS[1][:, 0:128], x1, k.J32, start=True, stop=True), reads=[tx1, k.tconst], writes=[tP[1]])
            p.op("dve", lambda e, c=c: e.tensor_copy(xr[:, c, :], PS[1][:, 0:128]), reads=[tP[1]], writes=[txr])
        p.dma("sp", ccX_in.rearrange("(c p) t -> p c t", p=128), xr, reads=[txr])
        p.barrier()
        p.cc(lambda e: e.collective_compute("AllGather", ALU.bypass, replica_groups=pairs, ins=[ccX_in.opt()], outs=[ccX_out.opt()]), writes=[tcc])
        p.dma("sp", G[:, 0], ccX_out[0:D, :].rearrange("(c p) t -> p c t", p=128), reads=[tcc], writes=[tG])
        p.dma("sp", G[:, 1], ccX_out[D:2 * D, :].rearrange("(c p) t -> p c t", p=128), reads=[tcc], writes=[tG])
        xh2 = xh.rearrange("p a b -> p (a b)")
        p.op("dve", lambda e: e.tensor_scalar(xh2, G[:, 0].rearrange("p a b -> p (a b)"), k.sel[:, 0:1], None, ALU.mult), reads=[tG, k.tconst], writes=[txh])
        p.op("dve", lambda e: e.scalar_tensor_tensor(xh2, G[:, 1].rearrange("p a b -> p (a b)"), k.sel[:, 1:2], xh2, ALU.mult, ALU.add), reads=[tG, k.tconst, txh], writes=[txh])
        p.dma("sp", xT_v[:, :, T:TE], xh, reads=[txh])
        p.barrier()

    for l in range(nlayers):
        phase_A(l)
        if stop_after == "A":
            break
        phase_B(l)
        if stop_after == "B":
            break
        phase_C1(l)
        if stop_after == "C1":
            break
        phase_C2(l)
        if stop_after == "C2":
            break
        phase_D(l)
        if stop_after == "D":
            break
        phase_E(l)
        if stop_after == "E":
            break
        phase_F(l, last=(l == nlayers - 1))
        if stop_after == "F":
            break
        if l < nlayers - 1:
            phase_X2()

    p.barrier()
    p.emit()
    LAST_PROG[0] = p
    return nc


def host_consts():
    kk = np.arange(128)[:, None]
    ll = np.arange(128)[None, :]
    ident = (kk == ll).astype(np.float32)
    triU = (kk <= ll).astype(np.float32)
    triL = (kk >= ll).astype(np.float32)
    maskA = np.where(ll >= kk, 0.0, NEG).astype(np.float32)
    maskB = np.where(kk >= ll, 0.0, NEG).astype(np.float32)
    ones = np.ones((128, 128), np.float32)
    Jm = (kk + ll == 127).astype(np.float32)
    cst = np.concatenate([ident, triU, triL, -triU, -triL, maskA, maskB, ones, Jm], axis=1)
    slopes = np.array([2.0 ** (-8.0 * (h + 1) / 16) for h in range(16)], np.float32)
    s = np.arange(128)[:, None, None, None]
    jj = np.arange(3)[None, :, None, None]
    q = np.arange(128)[None, None, None, :]
    rel = np.abs(q - (s + (jj - 1) * 128)).astype(np.float32)
    bias8 = np.where(rel <= 128, -8.0 * slopes[None, None, :, None] * rel, 8.0 * NEG).astype(np.float32)
    hi = bias8.astype(ml_dtypes.bfloat16).astype(np.float32)
    lo = (bias8 - hi).astype(ml_dtypes.bfloat16).astype(np.float32)
    em = np.concatenate([hi.reshape(128, 6144), lo.reshape(128, 6144)], axis=1)
    return cst, em


def gvec(g):
    return np.ascontiguousarray(np.asarray(g, np.float32).reshape(8, 128).T)


def prep_inputs(inputs, ncores=8, nlayers=2):
    I = {k_: np.asarray(v) for k_, v in inputs.items()}
    cst, em = host_consts()
    common = {"cst": cst, "em": em}
    per_par = [dict(), dict()]
    OFF_K, OFF_V, OFF_Z, OFF_XBC, OFF_DT, OFF_G = 1024, 1280, 1536, 3584, 7680, 7744
    for l in range(nlayers):
        w = I["w_in"][l]
        common["wfm%d" % l] = np.ascontiguousarray(np.concatenate([w[:, 0:1024], w[:, OFF_K:OFF_V], w[:, OFF_G:OFF_G + 2048], w[:, OFF_XBC:OFF_DT]], axis=1))
        common["wtm%d" % l] = np.ascontiguousarray(np.concatenate([w[:, OFF_V:OFF_Z], w[:, OFF_Z:OFF_XBC]], axis=1))
        for par in range(2):
            dd = per_par[par]
            order = [0, 1] if par == 0 else [1, 0]
            wdt = w[:, OFF_DT:OFF_G].reshape(1024, 2, 32)[:, order, :].reshape(1024, 64)
            dd["wdt%d" % l] = np.ascontiguousarray(wdt)
            cw = I["conv_w"][l]
            if par == 1:
                cw = cw[::-1]
            dd["cw%d" % l] = np.ascontiguousarray(cw.reshape(5, 32, 128).transpose(2, 1, 0).reshape(128, 160))
            dd["dtb%d" % l] = np.ascontiguousarray(I["dt_bias"][l][order].reshape(1, 64))
            dd["alog%d" % l] = np.ascontiguousarray(I["a_log"][l][order].reshape(1, 64))
        common["cb%d" % l] = np.ascontiguousarray(I["conv_b"][l].reshape(32, 128).T)
        common["cbrow%d" % l] = np.ascontiguousarray(I["conv_b"][l].reshape(1, 4096))
        common["dsk%d" % l] = np.ascontiguousarray(I["d_skip"][l].reshape(1, 32))
        common["sng%d" % l] = np.ascontiguousarray(I["ssd_norm"][l].reshape(1, 2048))
        common["sink%d" % l] = np.ascontiguousarray(I["attn_sink"][l].reshape(1, 16))
        common["wa%d" % l] = I["w_attn_branch"][l]
        common["ws%d" % l] = I["w_ssd_branch"][l]
        common["wo%d" % l] = I["w_out"][l]
        common["wxq%d" % l] = I["w_xq"][l]
        common["wxkv%d" % l] = I["w_xkv"][l]
        common["wxo%d" % l] = I["w_xo"][l]
        common["wup%d" % l] = I["w_up"][l]
        common["wdn%d" % l] = I["w_down"][l]
    gl = []
    for l in range(2):
        ll_ = min(l, nlayers - 1)
        gl += [gvec(I["norm_mix"][ll_]), gvec(I["norm_cross"][ll_]), gvec(I["norm_mem"][ll_]), gvec(I["norm_ffn"][ll_])]
    gl.append(gvec(I["norm_final"]))
    common["gn"] = np.ascontiguousarray(np.concatenate(gl, axis=1))
    in_maps = []
    for c in range(ncores):
        b, par = c // 2, c % 2
        xs = I["x"][b]
        if par == 1:
            xs = xs[::-1]
        m = dict(common)
        m.update(per_par[par])
        m["xT"] = np.ascontiguousarray(xs[:TE].T)
        m["memT"] = np.ascontiguousarray(I["mem"][b].T)
        m["sel"] = np.tile(np.array([[0.0, 1.0]] if par == 0 else [[1.0, 0.0]], np.float32), (128, 1))
        in_maps.append(m)
    return in_maps


_NC_CACHE = {}


def kernel(**inputs):
    ncores = 8
    in_maps = prep_inputs(inputs, ncores)
    if "nc" not in _NC_CACHE:
        _NC_CACHE["nc"] = build(ncores)
    nc = _NC_CACHE["nc"]
    res = run_bass_kernel_spmd(nc, in_maps, core_ids=list(range(ncores)))
    out = np.empty((4, 8192, D), np.float32)
    for c in range(ncores):
        b, par = c // 2, c % 2
        o = res.results[c]["outT"].T
        if par == 0:
            out[b, 0:T] = o
        else:
            out[b, T:] = o[::-1]
    return out
```

```python
import types
import numpy as np
import ml_dtypes
import concourse.bass as bass
import concourse.mybir as mybir
from concourse.bass_utils import run_bass_kernel_spmd

F32 = mybir.dt.float32
BF16 = mybir.dt.bfloat16
U8 = mybir.dt.uint8
AF = mybir.ActivationFunctionType
ALU = mybir.AluOpType

D = 1024
T = 4096
HALO = 128
TE = T + HALO
NB = T // 128
EPS = 1e-6
NEG = -30000.0
ARENA = 200 * 1024

ENGS = ("pe", "act", "dve", "pool", "sp")
CARVE_LOG = []
ARENA_HW = [0]
LAST_PROG = [None]


def _freeze(fn):
    if fn is None or fn.__closure__ is None:
        return fn
    cells = []
    for c in fn.__closure__:
        try:
            cells.append(types.CellType(c.cell_contents))
        except ValueError:
            cells.append(c)
    return types.FunctionType(fn.__code__, fn.__globals__, fn.__name__, fn.__defaults__, tuple(cells))


class Tk:
    __slots__ = ("w", "r", "banks")

    def __init__(self, banks=()):
        self.w = None
        self.r = []
        self.banks = tuple(banks)


class Prog:
    def __init__(self, nc, n_dma_sems=8):
        self.nc = nc
        self.sems = {}
        self.cnt = {}
        self.cur = {}
        self.phase = 0
        for e in ("pe", "act", "dve", "pool"):
            self.sems[e] = nc.alloc_semaphore("s_" + e)
            self.cnt[e] = 0
            self.cur[e] = e
        self.dma_sems = {}
        for q in ("sp", "pool"):
            self.dma_sems[q] = []
            for i in range(n_dma_sems):
                k = "d_%s%d" % (q, i)
                self.sems[k] = nc.alloc_semaphore(k)
                self.cnt[k] = 0
                self.dma_sems[q].append(k)
        self.sems["cc"] = nc.alloc_semaphore("s_cc")
        self.cnt["cc"] = 0
        self.dma_rr = {"sp": 0, "pool": 0}
        self.streams = {e: [] for e in ENGS}
        self.waited = {e: {} for e in ENGS}
        self.marks = []
        self.armarks = []
        self.bank_last = [dict() for _ in range(8)]

    def _deps(self, eng, reads, writes):
        need = {}

        def add(tok):
            if tok is None:
                return
            k, v = tok
            if k.startswith("pe") and eng == "pe":
                return
            if need.get(k, 0) < v:
                need[k] = v
        for t in reads:
            add(t.w)
        for t in writes:
            add(t.w)
            for tok in t.r:
                add(tok)
        for t in list(reads) + list(writes):
            for b in t.banks:
                for e2, tok in self.bank_last[b].items():
                    if e2 != eng:
                        add(tok)
        out = []
        wd = self.waited[eng]
        for k, v in need.items():
            if wd.get(k, 0) < v:
                wd[k] = v
                out.append((k, v))
        return out

    def _commit(self, tok, reads, writes, eng=None):
        for t in list(reads) + list(writes):
            for b in t.banks:
                self.bank_last[b][eng] = tok
        for t in reads:
            t.r.append(tok)
            if len(t.r) > 64:
                mx = {}
                for k, v in t.r:
                    if mx.get(k, 0) < v:
                        mx[k] = v
                t.r = list(mx.items())
        for t in writes:
            t.w = tok
            t.r = []

    def op(self, eng, fn, reads=(), writes=()):
        waits = self._deps(eng, reads, writes)
        key = self.cur[eng]
        self.cnt[key] += 1
        tok = (key, self.cnt[key])
        self._commit(tok, reads, writes, eng)
        self.streams[eng].append((waits, _freeze(fn), (key, 1)))
        return tok

    def dma(self, q, out, in_, reads=(), writes=()):
        waits = self._deps(q, reads, writes)
        lst = self.dma_sems[q]
        k = lst[self.dma_rr[q] % len(lst)]
        self.dma_rr[q] += 1
        self.cnt[k] += 16
        tok = (k, self.cnt[k])
        self._commit(tok, reads, writes)

        def fn(e, out=out, in_=in_):
            return e.dma_start(out=out, in_=in_)
        self.streams[q].append((waits, fn, (k, 16)))
        return tok

    def cc(self, fn, reads=(), writes=()):
        waits = self._deps("pool", reads, writes)
        self.cnt["cc"] += 1
        tok = ("cc", self.cnt["cc"])
        self._commit(tok, reads, writes)
        self.streams["pool"].append((waits, _freeze(fn), ("cc", 1)))
        return tok

    def barrier(self):
        self.marks.append(dict(self.cnt))
        self.armarks.append(ARENA_HW[0]); ARENA_HW[0] = 0
        for e in ENGS:
            waits = []
            for k, v in self.cnt.items():
                if k.startswith("pe") and e == "pe":
                    continue
                if v > self.waited[e].get(k, 0):
                    self.waited[e][k] = v
                    waits.append((k, v))
            self.streams[e].append((waits, None, None))
        self.phase += 1
        for e in ("pe", "act", "dve", "pool"):
            key = "%s@%d" % (e, self.phase)
            self.sems[key] = self.nc.alloc_semaphore("s_%s_%d" % (e, self.phase))
            self.cnt[key] = 0
            self.cur[e] = key

    def emit(self):
        nc = self.nc
        prog = self
        with nc.Block() as block:
            def mk(ename):
                def body(e):
                    for waits, fn, inc in prog.streams[ename]:
                        for k, v in waits:
                            e.wait_ge(prog.sems[k], v)
                        if fn is not None:
                            fn(e).then_inc(prog.sems[inc[0]], inc[1])
                return body
            block.tensor(mk("pe"))
            block.scalar(mk("act"))
            block.vector(mk("dve"))
            block.gpsimd(mk("pool"))
            block.sync(mk("sp"))


class Arena:
    def __init__(self, nc):
        self.t = nc.alloc_sbuf_tensor("arena", [128, ARENA], U8)
        self.off = 0

    def reset(self, to=0):
        self.off = to

    def carve(self, shape, dt):
        esz = 4 if dt == F32 else 2
        n = int(np.prod(shape[1:])) * esz
        assert self.off + n <= ARENA, ("arena overflow", self.off, n)
        v = self.t[:, self.off:self.off + n].bitcast(dt)
        CARVE_LOG.append((self.off, tuple(shape), str(dt)))
        self.off += (n + 63) // 64 * 64
        ARENA_HW[0] = max(ARENA_HW[0], self.off)
        if len(shape) == 3:
            v = v.rearrange("p (a b) -> p a b", a=shape[1])
        elif len(shape) == 4:
            v = v.rearrange("p (a b c) -> p a b c", a=shape[1], b=shape[2])
        return v


class Ring:
    def __init__(self, ar, n, shape, dt):
        self.aps = [ar.carve(shape, dt) for _ in range(n)]
        self.tk = [Tk() for _ in range(n)]
        self.n = n
        self.i = -1

    def next(self):
        self.i += 1
        j = self.i % self.n
        return self.aps[j], self.tk[j]


class K:
    pass


def _psum_banks(nc):
    pst = nc.alloc_psum_tensor("psum_all", [128, 4096], F32)
    banks = [pst[:, i * 512:(i + 1) * 512] for i in range(8)]
    return pst, banks


def rmsnorm_tile(k, xin, txin, n, g_ap, out_fn, sq, tsq, rst, trst, psb, tps, out_dt_eng=("dve", "pool")):
    p = k.p
    sqb = sq.bitcast(BF16)
    p.op("act", lambda e: e.activation(sqb[:, :, 0:n], xin[:, :, 0:n], AF.Square), reads=[txin], writes=[tsq])
    for c in range(8):
        p.op("pe", lambda e, c=c: e.matmul(psb[:, 0:n], k.onesb, sqb[:, c, 0:n], start=(c == 0), stop=(c == 7)),
             reads=[tsq, k.tconst], writes=[tps])
    p.op("act", lambda e: e.activation(rst[:, 0:n], psb[:, 0:n], AF.Ln, bias=k.epsc[:, 0:1], scale=1.0 / D), reads=[tps, k.tconst], writes=[trst])
    p.op("act", lambda e: e.activation(rst[:, 0:n], rst[:, 0:n], AF.Exp, scale=-0.5), reads=[trst], writes=[trst])
    for c in range(8):
        o, ot = out_fn(c)
        p.op("dve", lambda e, c=c, o=o: e.scalar_tensor_tensor(o, xin[:, c, 0:n], g_ap[:, c:c + 1], rst[:, 0:n], ALU.mult, ALU.mult),
             reads=[txin, trst, k.tconst], writes=ot)


def build(ncores=8, nlayers=2, dbg=(), stop_after=None, bvar=9):
    nc = bass.Bass("TRN2", target_bir_lowering=False)
    k = K()
    k.nc = nc
    p = k.p = Prog(nc)
    ar = k.ar = Arena(nc)
    PSALL, PS = _psum_banks(nc)
    k.ps = PS
    k.tps = [Tk(banks=(i,)) for i in range(8)]

    def din(name, shape, dt=F32):
        return nc.dram_tensor(name, list(shape), dt, kind="ExternalInput").ap()

    def dscr(name, shape, dt):
        kind = "ExternalOutput" if name in dbg else "Internal"
        return nc.dram_tensor(name, list(shape), dt, kind=kind).ap()

    xT_in = din("xT", [D, TE])
    memT_in = din("memT", [D, 256])
    sel_in = din("sel", [128, 2])
    cst_in = din("cst", [128, 1152])
    em_in = din("em", [128, 2 * 3 * 16 * 128])
    gn_in = din("gn", [128, 8 * 9])
    L = []
    for l in range(nlayers):
        d = {}
        d["wfm"] = din("wfm%d" % l, [D, 7424])
        d["wtm"] = din("wtm%d" % l, [D, 2304])
        d["wdt"] = din("wdt%d" % l, [D, 64])
        d["cw"] = din("cw%d" % l, [128, 32 * 5])
        d["cb"] = din("cb%d" % l, [128, 32])
        d["cbrow"] = din("cbrow%d" % l, [1, 4096])
        d["dtb"] = din("dtb%d" % l, [1, 64])
        d["alog"] = din("alog%d" % l, [1, 64])
        d["dsk"] = din("dsk%d" % l, [1, 32])
        d["sng"] = din("sng%d" % l, [1, 2048])
        d["sink"] = din("sink%d" % l, [1, 16])
        d["wa"] = din("wa%d" % l, [1024, 1024])
        d["ws"] = din("ws%d" % l, [2048, 1024])
        d["wo"] = din("wo%d" % l, [1024, 1024])
        d["wxq"] = din("wxq%d" % l, [1024, 512])
        d["wxkv"] = din("wxkv%d" % l, [1024, 1024])
        d["wxo"] = din("wxo%d" % l, [512, 1024])
        d["wup"] = din("wup%d" % l, [1024, 4096])
        d["wdn"] = din("wdn%d" % l, [4096, 1024])
        L.append(d)
    outT = nc.dram_tensor("outT", [D, T], F32, kind="ExternalOutput").ap()

    xT = dscr("xTs", [D, TE], F32)
    qT = dscr("qT", [64, 16, T], BF16)
    kdT = dscr("kdT", [64, 4, TE], BF16)
    gT = dscr("gT", [2048, T], BF16)
    uT = dscr("uT", [4096, TE + 4], BF16)
    vv = dscr("vv", [TE, 256], BF16)
    zz = dscr("zz", [T, 2048], BF16)
    dtr = dscr("dtr", [T, 64], F32)
    attnT = dscr("attnT", [64, 16, T], BF16)
    ssdT = dscr("ssdT", [2048, T], BF16)
    ypart = dscr("ypart", [T, 2048], F32)
    cTs = dscr("cTs", [1024, T], BF16)
    bsv = dscr("bsv", [T, 1024], BF16)
    xdsb = dscr("xdsb", [T, 2048], BF16)
    ecsv = dscr("ecsv", [T, 64], F32)
    cdcv = dscr("cdcv", [NB, 128, 64], F32)
    ccS_in = nc.dram_tensor("ccS_in", [128, 2048], F32).ap()
    ccS_out = nc.dram_tensor("ccS_out", [256, 2048], F32).ap()
    ccX_in = nc.dram_tensor("ccX_in", [D, HALO], F32).ap()
    ccX_out = nc.dram_tensor("ccX_out", [2 * D, HALO], F32).ap()
    k.tdram = Tk()
    pairs = [[2 * i, 2 * i + 1] for i in range(ncores // 2)]

    cbf = ar.carve([128, 1024], BF16)
    c32 = ar.carve([128, 640], F32)
    gn = ar.carve([128, 72], F32)
    sel = ar.carve([128, 2], F32)
    mhalf = ar.carve([128, 1], F32)
    epsc = ar.carve([128, 1], F32)
    k.epsc = epsc
    k.tconst = Tk()
    k.ident = cbf[:, 0:128]
    k.triU = cbf[:, 128:256]
    k.triL = cbf[:, 256:384]
    k.ntriU = cbf[:, 384:512]
    k.ntriL = cbf[:, 512:640]
    k.maskA = cbf[:, 640:768]
    k.maskB = cbf[:, 768:896]
    k.onesb = cbf[:, 896:1024]
    k.triU32 = c32[:, 0:128]
    k.triL32 = c32[:, 128:256]
    k.ones32 = c32[:, 256:384]
    k.ident32 = c32[:, 384:512]
    k.J32 = c32[:, 512:640]
    k.mhalf = mhalf
    k.gn = gn
    k.sel = sel
    p.dma("pool", cbf, cst_in[:, 0:1024], writes=[k.tconst])
    p.dma("sp", c32[:, 384:512], cst_in[:, 0:128], writes=[k.tconst])
    p.dma("sp", c32[:, 512:640], cst_in[:, 1024:1152], writes=[k.tconst])
    p.dma("sp", c32[:, 0:256], cst_in[:, 128:384], writes=[k.tconst])
    p.dma("sp", c32[:, 256:384], cst_in[:, 896:1024], writes=[k.tconst])
    p.dma("sp", gn, gn_in, writes=[k.tconst])
    p.dma("sp", sel, sel_in, writes=[k.tconst])
    p.op("pool", lambda e: e.memset(mhalf, -0.5), writes=[k.tconst])
    p.op("pool", lambda e: e.memset(epsc, EPS), writes=[k.tconst])
    p.dma("sp", xT, xT_in)
    base_off = ar.off
    p.barrier()

    xT_v = xT.rearrange("(c p) t -> p c t", p=128)

    def phase_A(l):
        W = L[l]
        ar.reset(base_off)
        hT = ar.carve([128, 8, TE], BF16)
        thT = [Tk() for _ in range(9)]
        xin = Ring(ar, 2, [128, 8, 512], F32)
        sq = Ring(ar, 1, [128, 8, 512], F32)
        rst = Ring(ar, 2, [128, 512], F32)
        wfm = Ring(ar, 2, [128, 8, 512], BF16)
        ost = Ring(ar, 2, [128, TE], BF16)
        ostm = Ring(ar, 3, [128, 512], F32)
        zero = ar.carve([128, 2], BF16)
        tz = Tk()
        g_ap = k.gn[:, (4 * l) * 8:(4 * l) * 8 + 8]
        tiles = [(i * 512, 512) for i in range(8)] + [(T, HALO)]
        p.op("pool", lambda e: e.memset(zero, 0.0), writes=[tz])
        for ch in range(32):
            pass
        for q_ in range(4):
            p.dma("sp", uT.rearrange("(c p) t -> p c t", p=128)[:, q_ * 8:(q_ + 1) * 8, 0:2], zero.unsqueeze(1).to_broadcast([128, 8, 2]), reads=[tz])
        for ti, (t0, n) in enumerate(tiles):
            xa, xt = xin.next()
            p.dma("sp", xa[:, :, 0:n], xT_v[:, :, t0:t0 + n], writes=[xt])
            sa, st = sq.next()
            ra, rt = rst.next()
            bi = ti % 2
            rmsnorm_tile(k, xa, xt, n, g_ap, lambda c, t0=t0, n=n, ti=ti: (hT[:, c, t0:t0 + n], [thT[ti]]),
                         sa, st, ra, rt, PS[bi], k.tps[bi])
        pb = 0
        ev = 0
        for cb4 in range(15):
            wa_, wt = wfm.next()
            nbk = 4 if cb4 < 14 else 2
            p.dma("pool", wa_[:, :, 0:nbk * 128], W["wfm"].rearrange("(c p) n -> p c n", p=128)[:, :, cb4 * 512:cb4 * 512 + nbk * 128], writes=[wt])
            for j in range(nbk):
                blk = cb4 * 4 + j
                if blk < 8:
                    kind, dst, ntl = "q", None, 8
                elif blk < 10:
                    kind, dst, ntl = "k", None, 9
                elif blk < 26:
                    kind, dst, ntl = "g", gT[(blk - 10) * 128:(blk - 9) * 128, :], 8
                else:
                    kind, dst, ntl = "u", uT[(blk - 26) * 128:(blk - 25) * 128, 2:2 + TE], 9
                oa, ot = ost.next()
                for ti in range(ntl):
                    t0, n = tiles[ti]
                    bank = 2 + (pb % 4)
                    pb += 1
                    for c in range(8):
                        p.op("pe", lambda e, c=c, bank=bank, t0=t0, n=n, wa_=wa_, j=j:
                             e.matmul(PS[bank][:, 0:n], wa_[:, c, j * 128:(j + 1) * 128], hT[:, c, t0:t0 + n], start=(c == 0), stop=(c == 7)),
                             reads=[wt, thT[ti]], writes=[k.tps[bank]])
                    if kind == "g":
                        p.op("act", lambda e, bank=bank, t0=t0, n=n, oa=oa: e.activation(oa[:, t0:t0 + n], PS[bank][:, 0:n], AF.Sigmoid),
                             reads=[k.tps[bank]], writes=[ot])
                    else:
                        eng = "act" if (ev % 2 == 0) else "dve"
                        ev += 1
                        if eng == "act":
                            p.op("act", lambda e, bank=bank, t0=t0, n=n, oa=oa: e.activation(oa[:, t0:t0 + n], PS[bank][:, 0:n], AF.Copy),
                                 reads=[k.tps[bank]], writes=[ot])
                        else:
                            p.op("dve", lambda e, bank=bank, t0=t0, n=n, oa=oa: e.tensor_copy(oa[:, t0:t0 + n], PS[bank][:, 0:n]),
                                 reads=[k.tps[bank]], writes=[ot])
                ntok = T if ntl == 8 else TE
                if kind == "q":
                    p.dma("sp", qT[:, 2 * blk, :], oa[0:64, 0:T], reads=[ot])
                    p.dma("sp", qT[:, 2 * blk + 1, :], oa[64:128, 0:T], reads=[ot])
                elif kind == "k":
                    p.dma("sp", kdT[:, 2 * (blk - 8), :], oa[0:64, 0:TE], reads=[ot])
                    p.dma("sp", kdT[:, 2 * (blk - 8) + 1, :], oa[64:128, 0:TE], reads=[ot])
                else:
                    p.dma("sp", dst[:, 0:ntok], oa[:, 0:ntok], reads=[ot])
        groups = [("v", W["wtm"][:, 0:256], 256, 33)] + \
                 [("z%d" % i, W["wtm"][:, 256 + i * 512:256 + (i + 1) * 512], 512, 32) for i in range(4)] + \
                 [("dt", W["wdt"], 64, 32)]
        for gi, (name, wsrc, ncol, nblk) in enumerate(groups):
            wa_, wt = wfm.next()
            p.dma("pool", wa_[:, :, 0:ncol], wsrc.rearrange("(c p) n -> p c n", p=128), writes=[wt])
            for tb in range(nblk):
                bank = 2 + (pb % 4)
                pb += 1
                ti = min(tb // 4, 8)
                for c in range(8):
                    p.op("pe", lambda e, c=c, bank=bank, tb=tb, wa_=wa_, ncol=ncol:
                         e.matmul(PS[bank][:, 0:ncol], hT[:, c, tb * 128:(tb + 1) * 128], wa_[:, c, 0:ncol], start=(c == 0), stop=(c == 7)),
                         reads=[wt, thT[ti]], writes=[k.tps[bank]])
                oa, ot = ostm.next()
                if name == "dt":
                    ov = oa[:, 0:64]
                    dst = dtr[tb * 128:(tb + 1) * 128, :]
                elif name == "v":
                    ov = oa.bitcast(BF16)[:, 0:256]
                    dst = vv[tb * 128:(tb + 1) * 128, :]
                else:
                    zi = int(name[1:])
                    ov = oa.bitcast(BF16)[:, 0:512]
                    dst = zz[tb * 128:(tb + 1) * 128, zi * 512:(zi + 1) * 512]
                eng = "act" if (ev % 2 == 0) else "dve"
                ev += 1
                if eng == "act":
                    p.op("act", lambda e, bank=bank, ov=ov, ncol=ncol: e.activation(ov, PS[bank][:, 0:ncol], AF.Copy),
                         reads=[k.tps[bank]], writes=[ot])
                else:
                    p.op("dve", lambda e, bank=bank, ov=ov, ncol=ncol: e.tensor_copy(ov, PS[bank][:, 0:ncol]),
                         reads=[k.tps[bank]], writes=[ot])
                p.dma("sp", dst, ov, reads=[ot])
        p.barrier()

    def bcast_row(ap2d, n):
        return ap2d.rearrange("a n -> (a n)").partition_broadcast(128)

    def pipelined(n, stages, skew=1):
        for ts_ in range(n + (len(stages) - 1) * skew):
            for si, f in enumerate(stages):
                i_ = ts_ - si * skew
                if 0 <= i_ < n:
                    f(i_)

    def phase_B(l):
        W = L[l]
        ar.reset(base_off)
        kd_sb = ar.carve([128, 4, TE], BF16)
        vones = ar.carve([128, 33, 4, 128], BF16)
        emb = ar.carve([128, 2, 6144], BF16)
        esink = ar.carve([128, 16], F32)
        qr = Ring(ar, 3, [128, 16, 128], BF16)
        pt = Ring(ar, 3, [128, 3, 512], BF16)
        dn = Ring(ar, 2, [128, 512], F32)
        ob = Ring(ar, 2, [128, 16, 128], BF16)
        tk_, tv_, te_, ts_ = Tk(), Tk(), Tk(), Tk()
        p.dma("sp", kd_sb[0:64], kdT, writes=[tk_])
        for g in range(4):
            for b3 in range(3):
                p.dma("sp", vones[:, b3 * 11:(b3 + 1) * 11, g, 0:64], vv.rearrange("(b p) c -> p b c", p=128)[:, b3 * 11:(b3 + 1) * 11, g * 64:(g + 1) * 64], writes=[tv_])
        p.op("pool", lambda e: e.memset(vones[:, :, :, 64:128], 1.0), writes=[tv_])
        for hl in range(2):
            p.dma("pool", emb[:, hl, :], em_in[:, hl * 6144:(hl + 1) * 6144], writes=[te_])
        p.dma("sp", esink, bcast_row(W["sink"], 16), writes=[ts_])
        p.op("act", lambda e: e.activation(esink, esink, AF.Exp), reads=[ts_], writes=[ts_])
        ctx = {}

        def stA(it):
            i, g = it // 4, it % 4
            if g == 0:
                qa, qt = qr.next()
                p.dma("sp", qa[0:64], qT[:, :, i * 128:(i + 1) * 128], writes=[qt])
                ctx[("q", i)] = (qa, qt)
                ctx[("o", i)] = ob.next()
            qa, qt = ctx[("q", i)]
            jl = [jj for jj in range(3) if 0 <= i - 1 + jj <= 32]
            jlo, jhi = jl[0], jl[-1] + 1
            sset = (it % 2) * 3
            for jj in jl:
                j = i - 1 + jj
                S3 = PS[sset + jj].rearrange("p (r q) -> p r q", r=4)
                for hl in range(2):
                    p.op("pe", lambda e: e.matmul(S3, k.ident, emb[:, hl, jj * 2048 + g * 512:jj * 2048 + (g + 1) * 512].rearrange("p (r q) -> p r q", r=4), start=(hl == 0), stop=False),
                         reads=[te_, k.tconst], writes=[k.tps[sset + jj]])
                for r in range(4):
                    h = 4 * g + r
                    p.op("pe", lambda e: e.matmul(PS[sset + jj][:, r * 128:(r + 1) * 128], kd_sb[0:64, g, j * 128:(j + 1) * 128],
                                                  qa[0:64, h, :], start=False, stop=(r == 3)),
                         reads=[tk_, qt], writes=[k.tps[sset + jj]])
            pa, ptk = pt.next()
            s3 = PSALL[:, sset * 512:(sset + 3) * 512].rearrange("p (j n) -> p j n", j=3)
            p.op("act", lambda e: e.activation(pa[:, jlo:jhi, :], s3[:, jlo:jhi, :], AF.Exp, scale=0.125),
                 reads=[k.tps[sset + jj] for jj in jl], writes=[ptk])
            ctx[("p", it)] = (pa, ptk, jl)

        def stB(it):
            i, g = it // 4, it % 4
            pa, ptk, jl = ctx.pop(("p", it))
            oa, ot = ctx[("o", i)]
            pvb = 6 + (it % 2)
            for n_, jj in enumerate(jl):
                j = i - 1 + jj
                p.op("pe", lambda e: e.matmul(PS[pvb][:, :], vones[:, j, g, :], pa[:, jj, :], start=(n_ == 0), stop=(n_ == len(jl) - 1)),
                     reads=[tv_, ptk], writes=[k.tps[pvb]])
            da, dt_ = dn.next()
            p.op("dve", lambda e: e.tensor_tensor(da[64:128, :].rearrange("p (r q) -> p r q", r=4), PS[pvb][64:128, :].rearrange("p (r q) -> p r q", r=4),
                                                  esink[64:128, 4 * g:4 * g + 4].unsqueeze(2).to_broadcast([64, 4, 128]), ALU.add),
                 reads=[k.tps[pvb], ts_], writes=[dt_])
            p.op("act", lambda e: e.activation(da[64:128, :], da[64:128, :], AF.Ln), reads=[dt_], writes=[dt_])
            p.op("act", lambda e: e.activation(da[64:128, :], da[64:128, :], AF.Exp, scale=-1.0), reads=[dt_], writes=[dt_])
            p.op("dve", lambda e: e.tensor_tensor(oa[0:64, 4 * g:4 * g + 4, :], PS[pvb][0:64, :].rearrange("p (r q) -> p r q", r=4),
                                                  da[64:128, :].rearrange("p (r q) -> p r q", r=4), ALU.mult),
                 reads=[k.tps[pvb], dt_], writes=[ot])
            if g == 3:
                p.dma("sp", attnT[:, :, i * 128:(i + 1) * 128], oa[0:64, :, :], reads=[ot])

        pipelined(NB * 4, [stA, stB])
        p.barrier()

    def phase_C1(l):
        W = L[l]
        ar.reset(base_off)
        dgs = ar.carve([128, 32, 5, 128], BF16)
        cw = ar.carve([128, 32, 5], F32)
        cbrow = ar.carve([128, 4096], BF16)
        dtb = ar.carve([128, 64], F32)
        arep = ar.carve([128, 64], F32)
        dsk = ar.carve([128, 32], F32)
        SA = ar.carve([128, 2048], F32)
        SAbf = [ar.carve([128, 2048], BF16) for _ in range(3)]
        tSA = [Tk() for _ in range(8)]
        tSAbf = [[Tk() for _ in range(8)] for _ in range(3)]
        uar = Ring(ar, 2, [128, 32, 132], BF16)
        dar = Ring(ar, 2, [128, 64], F32)
        xsr = Ring(ar, 3, [128, 3072], BF16)
        bcr = Ring(ar, 3, [128, 16, 128], BF16)
        smr = Ring(ar, 2, [128, 10, 64], F32)
        ahr = Ring(ar, 2, [128, 2, 64], BF16)
        xdr = Ring(ar, 2, [128, 5, 2048], BF16)
        er = Ring(ar, 2, [128, 512], F32)
        wdr = Ring(ar, 4, [128, 512], BF16)
        tmr = Ring(ar, 2, [128, 256], F32)
        ypr = Ring(ar, 2, [128, 2048], F32)
        tc_ = Tk()
        p.dma("sp", cw.rearrange("p a b -> p (a b)"), W["cw"], writes=[tc_])
        p.dma("pool", cbrow[0:1, :], W["cbrow"], writes=[tc_])
        p.dma("sp", dtb, bcast_row(W["dtb"], 64), writes=[tc_])
        p.dma("sp", arep, bcast_row(W["alog"], 64), writes=[tc_])
        p.dma("sp", dsk, bcast_row(W["dsk"], 32), writes=[tc_])
        p.op("act", lambda e: e.activation(arep, arep, AF.Exp), reads=[tc_], writes=[tc_])
        p.op("dve", lambda e: e.tensor_scalar(arep, arep, -1.0, None, ALU.mult), reads=[tc_], writes=[tc_])
        p.op("pool", lambda e: e.memset(SA, 0.0), writes=tSA)
        p.op("pool", lambda e: e.memset(SAbf[0], 0.0), writes=tSAbf[0])
        tdg = Tk()
        for ch in range(32):
            for kk in range(5):
                p.op("dve", lambda e, ch=ch, kk=kk: e.tensor_scalar(dgs[:, ch, kk, :], k.ident, cw[:, ch, kk:kk + 1], None, ALU.mult),
                     reads=[tc_, k.tconst], writes=[tdg])
        uT_v = uT.rearrange("(c p) t -> p c t", p=128)
        tconv = [k.tps[0], k.tps[1]]
        tcs = k.tps[2]
        tcbt = k.tps[3]
        tD = [k.tps[4], k.tps[5]]
        tY = k.tps[6]
        tYo = k.tps[7]
        tst = k.tps[7]
        cbr = Ring(ar, 2, [128, 128], F32)
        cvi = 0
        ones_row = k.onesb[0:1, 0:128]
        ctx = {}
        cvi_ = [0]

        def stA1(c):
            cvi = cvi_[0]
            ua, ut = uar.next()
            for q_ in range(4):
                p.dma("sp", ua[:, q_ * 8:(q_ + 1) * 8, :], uT_v[:, q_ * 8:(q_ + 1) * 8, c * 128:c * 128 + 132], writes=[ut])
            da, dat = dar.next()
            p.dma("sp", da, dtr[c * 128:(c + 1) * 128, :], writes=[dat])
            xs, xst = xsr.next()
            bc, bct = bcr.next()
            for q4 in range(6):
                bank = cvi % 2
                cvi += 1
                for j in range(4):
                    ch = q4 * 4 + j
                    o_ = PS[bank][:, j * 128:(j + 1) * 128]
                    for kk in range(5):
                        p.op("pe", lambda e, o_=o_, ua=ua, ch=ch, kk=kk: e.matmul(o_, ua[:, ch, kk:kk + 128], dgs[:, ch, kk, :], start=(kk == 0), stop=False),
                             reads=[ut, tdg], writes=[tconv[bank]])
                    p.op("pe", lambda e, o_=o_, ch=ch: e.matmul(o_, ones_row, cbrow[0:1, ch * 128:(ch + 1) * 128], start=False, stop=True),
                         reads=[tc_, k.tconst], writes=[tconv[bank]])
                p.op("act", lambda e, bank=bank, xs=xs, q4=q4: e.activation(xs[:, q4 * 512:(q4 + 1) * 512], PS[bank], AF.Silu),
                     reads=[tconv[bank]], writes=[xst])
            for q4 in range(4):
                bank = cvi % 2
                cvi += 1
                for j in range(4):
                    ch = 16 + q4 * 4 + j
                    o_ = PS[bank][:, j * 128:(j + 1) * 128]
                    for kk in range(5):
                        p.op("pe", lambda e, o_=o_, ua=ua, ch=ch, kk=kk: e.matmul(o_, dgs[:, ch, kk, :], ua[:, ch, kk:kk + 128], start=(kk == 0), stop=False),
                             reads=[ut, tdg], writes=[tconv[bank]])
                    p.op("pe", lambda e, o_=o_, ch=ch: e.matmul(o_, cbrow[0:1, ch * 128:(ch + 1) * 128], ones_row, start=False, stop=True),
                         reads=[tc_, k.tconst], writes=[tconv[bank]])
                p.op("act", lambda e, bank=bank, bc=bc, q4=q4: e.activation(bc[:, q4 * 4:(q4 + 1) * 4, :].rearrange("p a b -> p (a b)"), PS[bank], AF.Silu),
                     reads=[tconv[bank]], writes=[bct])
            BT = bc[:, 0:8, :]
            CT = bc[:, 8:16, :]
            Btm = xs[:, 2048:3072]
            cvi_[0] = cvi
            ctx[("a1", c)] = (da, dat, xs, xst, bc, bct, BT, CT, Btm)

        def stA2(c):
            da, dat, xs, xst, bc, bct, BT, CT, Btm = ctx.pop(("a1", c))
            sm, smt = smr.next()
            ah, aht = ahr.next()
            t1, dtv, a32, cs, ecs, cdec, dd, dstt, wv = [sm[:, i_, :] for i_ in range(9)]
            ahi, alo = ah[:, 0, :], ah[:, 1, :]
            p.op("dve", lambda e, t1=t1, da=da: e.tensor_tensor(t1, da, dtb, ALU.add), reads=[dat, tc_], writes=[smt])
            p.op("act", lambda e, t1=t1: e.activation(t1, t1, AF.Exp), reads=[smt], writes=[smt])
            p.op("act", lambda e, t1=t1, dtv=dtv: e.activation(dtv, t1, AF.Ln, bias=1.0), reads=[smt], writes=[smt])
            p.op("dve", lambda e, a32=a32, dtv=dtv: e.tensor_tensor(a32, dtv, arep, ALU.mult), reads=[smt, tc_], writes=[smt])
            p.op("dve", lambda e, ahi=ahi, a32=a32: e.tensor_copy(ahi, a32), reads=[smt], writes=[aht])
            p.op("dve", lambda e, alo=alo, a32=a32, ahi=ahi: e.tensor_tensor(alo, a32, ahi, ALU.subtract), reads=[smt, aht], writes=[aht])
            p.op("pe", lambda e, a32=a32: e.matmul(PS[2][:, 0:32], k.triU32, a32[:, 0:32], start=True, stop=True), reads=[smt, k.tconst], writes=[tcs])
            p.op("pe", lambda e, a32=a32: e.matmul(PS[2][:, 32:64], k.triL32, a32[:, 32:64], start=True, stop=True), reads=[smt, k.tconst], writes=[tcs])
            p.op("pe", lambda e, a32=a32: e.matmul(PS[2][:, 64:128], k.ones32, a32[:, 0:64], start=True, stop=True), reads=[smt, k.tconst], writes=[tcs])
            p.op("act", lambda e, cs=cs: e.activation(cs, PS[2][:, 0:64], AF.Copy), reads=[tcs], writes=[smt])
            p.op("act", lambda e, ecs=ecs: e.activation(ecs, PS[2][:, 0:64], AF.Exp), reads=[tcs], writes=[smt])
            p.op("act", lambda e, cdec=cdec: e.activation(cdec, PS[2][:, 64:128], AF.Exp), reads=[tcs], writes=[smt])
            p.op("dve", lambda e, dd=dd, cs=cs: e.tensor_tensor(dd, PS[2][:, 64:128], cs, ALU.subtract), reads=[tcs, smt], writes=[smt])
            p.op("act", lambda e, dd=dd, dstt=dstt: e.activation(dstt, dd, AF.Exp), reads=[smt], writes=[smt])
            p.op("dve", lambda e, wv=wv, dtv=dtv, dstt=dstt: e.tensor_tensor(wv, dtv, dstt, ALU.mult), reads=[smt], writes=[smt])
            xd, xdt_ = xdr.next()
            xs3 = xs[:, 0:2048].rearrange("p (h q) -> p h q", h=32)

            def bcm(eng, idx, src32, reads):
                p.op(eng, lambda e, idx=idx, src32=src32: e.tensor_tensor(xd[:, idx, :].rearrange("p (h q) -> p h q", h=32), xs3,
                                                                          src32.unsqueeze(2).to_broadcast([128, 32, 64]), ALU.mult),
                     reads=[xst] + reads, writes=[xdt_])
            bcm("dve", 0, dtv[:, 0:32], [smt])
            bcm("pool", 1, dtv[:, 32:64], [smt])
            bcm("dve", 2, wv[:, 0:32], [smt])
            bcm("pool", 3, wv[:, 32:64], [smt])
            bcm("pool", 4, dsk, [tc_])
            ctx[c] = (xs, xst, bc, bct, BT, CT, Btm, ecs, cdec, smt, ahi, alo, aht, xd, xdt_)

        def stB(c):
            xs, xst, bc, bct, BT, CT, Btm, ecs, cdec, smt, ahi, alo, aht, xd, xdt_ = ctx.pop(c)
            yp, ypt = ypr.next()
            cur, nxt = c % 3, (c + 1) % 3
            wdsd = {}
            for g in range(8):
                stp = PS[7][:, 256:512]
                p.op("pe", lambda e: e.matmul(stp, Btm[:, g * 128:(g + 1) * 128], xd[:, 2, g * 256:(g + 1) * 256], start=True, stop=True),
                     reads=[xst, xdt_], writes=[tst])
                SAg = SA[:, g * 256:(g + 1) * 256]
                p.op("pool", lambda e: e.tensor_tensor(SAg.rearrange("p (r q) -> p r q", r=4), SAg.rearrange("p (r q) -> p r q", r=4),
                                                       cdec[:, 4 * g:4 * g + 4].unsqueeze(2).to_broadcast([128, 4, 64]), ALU.mult),
                     reads=[smt, tSA[g]], writes=[tSA[g]])
                p.op("dve", lambda e: e.tensor_tensor(SAg, stp, SAg, ALU.add), reads=[tst, tSA[g]], writes=[tSA[g]])
                p.op("act", lambda e: e.activation(SAbf[nxt][:, g * 256:(g + 1) * 256], SAg, AF.Copy),
                     reads=[tSA[g]], writes=[tSAbf[nxt][g]])

            def S1(g):
                cbt_ps = PS[3][:, 0:128]
                p.op("pe", lambda e, cbt_ps=cbt_ps, BT=BT, CT=CT, g=g: e.matmul(cbt_ps, BT[:, g, :], CT[:, g, :], start=True, stop=True),
                     reads=[bct], writes=[tcbt])
                cbs, cbst = cbr.next()
                p.op("act", lambda e: e.activation(cbs, cbt_ps, AF.Copy), reads=[tcbt], writes=[cbst])
                wds = []
                for di in range(2):
                    base = 32 * di + 4 * g
                    tri = k.triU if di == 0 else k.triL
                    ntri = k.ntriU if di == 0 else k.ntriL
                    msk = k.maskA if di == 0 else k.maskB
                    Db = PS[4 + di]
                    Db3 = Db.rearrange("p (r l) -> p r l", r=4)
                    for hl, src in enumerate((ahi, alo)):
                        p.op("pe", lambda e, Db3=Db3, src=src, base=base, ntri=ntri, hl=hl:
                             e.matmul(Db3, ntri, src[:, base:base + 4].unsqueeze(2).to_broadcast([128, 4, 128]), start=(hl == 0), stop=False),
                             reads=[aht, k.tconst], writes=[tD[di]])
                    for r in range(4):
                        for hl, src in enumerate((ahi, alo)):
                            p.op("pe", lambda e, Db=Db, r=r, src=src, base=base, tri=tri, hl=hl:
                                 e.matmul(Db[:, r * 128:(r + 1) * 128], src[:, base + r:base + r + 1].to_broadcast([128, 128]), tri, start=False, stop=False),
                                 reads=[aht, k.tconst], writes=[tD[di]])
                    p.op("pe", lambda e, Db3=Db3, msk=msk: e.matmul(Db3, k.ident, msk.unsqueeze(1).to_broadcast([128, 4, 128]), start=False, stop=True),
                         reads=[k.tconst], writes=[tD[di]])
                    ea, eat = er.next()
                    p.op("act", lambda e, ea=ea, Db=Db: e.activation(ea, Db, AF.Exp), reads=[tD[di]], writes=[eat])
                    wd, wdt_ = wdr.next()
                    p.op("dve" if di == 0 else "pool", lambda e, wd=wd, ea=ea, cbs=cbs:
                         e.tensor_tensor(wd.rearrange("p (r l) -> p r l", r=4), ea.rearrange("p (r l) -> p r l", r=4),
                                         cbs.unsqueeze(1).to_broadcast([128, 4, 128]), ALU.mult),
                         reads=[eat, cbst], writes=[wdt_])
                    wds.append((wd, wdt_))
                wdsd[g] = wds

            def S2(g):
                wds = wdsd.pop(g)
                Yb = PS[6][:, 0:256]
                Yo = PS[7][:, 0:256]
                p.op("pe", lambda e, Yb=Yb, xd=xd, g=g: e.matmul(Yb, k.ident, xd[:, 4, g * 256:(g + 1) * 256], start=True, stop=False),
                     reads=[xdt_, k.tconst], writes=[tY])
                for di in range(2):
                    wd, wdt_ = wds[di]
                    for r in range(4):
                        hh = 4 * g + r
                        p.op("pe", lambda e, Yb=Yb, wd=wd, xd=xd, di=di, r=r, hh=hh:
                             e.matmul(Yb[:, r * 64:(r + 1) * 64], wd[:, r * 128:(r + 1) * 128], xd[:, di, hh * 64:(hh + 1) * 64],
                                      start=False, stop=(di == 1 and r == 3)),
                             reads=[wdt_, xdt_], writes=[tY])
                p.op("pe", lambda e, Yo=Yo, CT=CT, g=g, cur=cur: e.matmul(Yo, CT[:, g, :], SAbf[cur][:, g * 256:(g + 1) * 256], start=True, stop=True),
                     reads=[bct, tSAbf[cur][g]], writes=[tYo])
                tm, tmt = tmr.next()
                p.op("dve", lambda e, tm=tm, Yo=Yo, ecs=ecs, g=g:
                     e.tensor_tensor(tm.rearrange("p (r q) -> p r q", r=4), Yo.rearrange("p (r q) -> p r q", r=4),
                                     ecs[:, 4 * g:4 * g + 4].unsqueeze(2).to_broadcast([128, 4, 64]), ALU.mult),
                     reads=[tYo, smt], writes=[tmt])
                p.op("dve", lambda e, yp=yp, Yb=Yb, tm=tm, g=g: e.tensor_tensor(yp[:, g * 256:(g + 1) * 256], Yb, tm, ALU.add),
                     reads=[tY, tmt], writes=[ypt])
            S1(0)
            for g in range(8):
                if g + 1 < 8:
                    S1(g + 1)
                S2(g)
            p.dma("sp", ypart[c * 128:(c + 1) * 128, :], yp, reads=[ypt])
            p.dma("sp", cTs.rearrange("(g p) t -> p g t", p=128)[:, :, c * 128:(c + 1) * 128], CT, reads=[bct])
            p.dma("sp", bsv[c * 128:(c + 1) * 128, :], Btm, reads=[xst])
            p.dma("sp", xdsb[c * 128:(c + 1) * 128, :], xd[:, 3, :], reads=[xdt_])
            p.dma("sp", ecsv[c * 128:(c + 1) * 128, :], ecs, reads=[smt])
            p.dma("sp", cdcv[c], cdec, reads=[smt])
        pipelined(NB, [stA1, stA2, stB])
        p.dma("sp", ccS_in, SA, reads=tSA)
        p.barrier()

    def phase_C2(l):
        W = L[l]
        ar.reset(base_off)
        SB = ar.carve([128, 2048], F32)
        SBbf = [ar.carve([128, 2048], BF16) for _ in range(3)]
        G = ar.carve([128, 2, 2048], F32)
        ngr = ar.carve([128, 2048], F32)
        tSB = [Tk() for _ in range(8)]
        tSBbf = [[Tk() for _ in range(8)] for _ in range(3)]
        tG, tng, tcc = Tk(), Tk(), Tk()
        ypr = Ring(ar, 3, [128, 2048], F32)
        xdr = Ring(ar, 2, [128, 2048], BF16)
        btr = Ring(ar, 2, [128, 1024], BF16)
        ctr = Ring(ar, 3, [128, 8, 128], BF16)
        ecr = Ring(ar, 3, [128, 2, 64], F32)
        ztr = Ring(ar, 3, [128, 2048], BF16)
        szr = Ring(ar, 1, [128, 2048], F32)
        yor = Ring(ar, 2, [128, 2048], BF16)
        str_ = Ring(ar, 2, [128, 16, 128], BF16)
        tmr = Ring(ar, 2, [128, 256], F32)
        ssr = Ring(ar, 2, [128, 16], F32)
        jk = ar.carve([128, 256], F32)
        tjk = Tk()
        p.cc(lambda e: e.collective_compute("AllGather", ALU.bypass, replica_groups=pairs, ins=[ccS_in.opt()], outs=[ccS_out.opt()]),
             writes=[tcc])
        p.dma("sp", G, ccS_out.rearrange("(r p) f -> p r f", p=128), reads=[tcc], writes=[tG])
        p.dma("sp", ngr, bcast_row(W["sng"], 2048), writes=[tng])
        p.op("dve", lambda e: e.tensor_scalar(SB, G[:, 0, :], k.sel[:, 0:1], None, ALU.mult), reads=[tG, k.tconst], writes=tSB)
        p.op("dve", lambda e: e.scalar_tensor_tensor(SB, G[:, 1, :], k.sel[:, 1:2], SB, ALU.mult, ALU.add), reads=[tG, k.tconst] + tSB, writes=tSB)
        p.op("act", lambda e: e.activation(SBbf[0], SB, AF.Copy), reads=tSB, writes=tSBbf[0])
        tYo = [k.tps[0], k.tps[4]]
        tst = [k.tps[1], k.tps[5]]
        ttr = [k.tps[2], k.tps[3]]
        PSB = PSALL.bitcast(BF16)
        ctx = {}
        itc = [0]

        def stA(ci):
            c = NB - 1 - ci
            xdb, xdbt = xdr.next()
            p.dma("sp", xdb, xdsb[c * 128:(c + 1) * 128, :], writes=[xdbt])
            bt_, btt = btr.next()
            p.dma("sp", bt_, bsv[c * 128:(c + 1) * 128, :], writes=[btt])
            ec, ect = ecr.next()
            p.dma("sp", ec[:, 0, :], ecsv[c * 128:(c + 1) * 128, :], writes=[ect])
            p.dma("sp", ec[:, 1, :], cdcv[c], writes=[ect])
            ct_, ctt = ctr.next()
            p.dma("sp", ct_, cTs.rearrange("(g p) t -> p g t", p=128)[:, :, c * 128:(c + 1) * 128], writes=[ctt])
            yp, ypt = ypr.next()
            p.dma("sp", yp, ypart[c * 128:(c + 1) * 128, :], writes=[ypt])
            zt, ztt = ztr.next()
            p.dma("sp", zt, zz[c * 128:(c + 1) * 128, :], writes=[ztt])
            nxt = (ci + 1) % 3
            for g in range(8):
                hb = g % 2
                stp = PS[1 if hb == 0 else 5][:, 0:256]
                p.op("pe", lambda e: e.matmul(stp, bt_[:, g * 128:(g + 1) * 128], xdb[:, g * 256:(g + 1) * 256], start=True, stop=True),
                     reads=[btt, xdbt], writes=[tst[hb]])
                SBg = SB[:, g * 256:(g + 1) * 256]
                p.op("pool", lambda e: e.tensor_tensor(SBg.rearrange("p (r q) -> p r q", r=4), SBg.rearrange("p (r q) -> p r q", r=4),
                                                       ec[:, 1, 32 + 4 * g:32 + 4 * g + 4].unsqueeze(2).to_broadcast([128, 4, 64]), ALU.mult),
                     reads=[ect, tSB[g]], writes=[tSB[g]])
                p.op("dve", lambda e: e.tensor_tensor(SBg, stp, SBg, ALU.add), reads=[tst[hb], tSB[g]], writes=[tSB[g]])
                p.op("act", lambda e: e.activation(SBbf[nxt][:, g * 256:(g + 1) * 256], SBg, AF.Copy),
                     reads=[tSB[g]], writes=[tSBbf[nxt][g]])
            ctx[ci] = (c, yp, ypt, zt, ztt, ct_, ctt, ec, ect)

        def stB1(ci):
            c, yp, ypt, zt, ztt, ct_, ctt, ec, ect = ctx.pop(ci)
            cur = ci % 3
            sz, szt = szr.next()
            p.op("act", lambda e: e.activation(sz, zt, AF.Silu), reads=[ztt], writes=[szt])
            for g in range(8):
                hb = g % 2
                Yo = PS[0 if hb == 0 else 4][:, 0:256]
                p.op("pe", lambda e: e.matmul(Yo, ct_[:, g, :], SBbf[cur][:, g * 256:(g + 1) * 256], start=True, stop=True),
                     reads=[ctt, tSBbf[cur][g]], writes=[tYo[hb]])
                tm, tmt = tmr.next()
                p.op("dve", lambda e: e.tensor_tensor(tm.rearrange("p (r q) -> p r q", r=4), Yo.rearrange("p (r q) -> p r q", r=4),
                                                      ec[:, 0, 32 + 4 * g:32 + 4 * g + 4].unsqueeze(2).to_broadcast([128, 4, 64]), ALU.mult),
                     reads=[tYo[hb], ect], writes=[tmt])
                ypg = yp[:, g * 256:(g + 1) * 256]
                p.op("dve", lambda e: e.tensor_tensor(ypg, ypg, tm, ALU.add), reads=[tmt, ypt], writes=[ypt])
            p.op("dve", lambda e: e.tensor_tensor(yp, yp, sz, ALU.mult), reads=[ypt, szt], writes=[ypt])
            ctx[("b1", ci)] = (c, yp, ypt)

        def stB2(ci):
            c, yp, ypt = ctx.pop(("b1", ci))
            ss, sst = ssr.next()
            for g in range(8):
                p.op("act", lambda e: e.activation(jk, yp[:, g * 256:(g + 1) * 256], AF.Square, accum_out=ss[:, g:g + 1]),
                     reads=[ypt], writes=[tjk, sst])
            p.op("act", lambda e: e.activation(ss[:, 8:16], ss[:, 0:8], AF.Ln, bias=k.epsc[:, 0:1], scale=1.0 / 256), reads=[sst, k.tconst], writes=[sst])
            p.op("act", lambda e: e.activation(ss[:, 8:16], ss[:, 8:16], AF.Exp, scale=-0.5), reads=[sst], writes=[sst])
            yo, yot = yor.next()
            for g in range(8):
                p.op("dve", lambda e: e.scalar_tensor_tensor(yo[:, g * 256:(g + 1) * 256], yp[:, g * 256:(g + 1) * 256], ss[:, 8 + g:9 + g],
                                                             ngr[:, g * 256:(g + 1) * 256], ALU.mult, ALU.mult),
                     reads=[ypt, sst, tng], writes=[yot])
            ctx[("b2", ci)] = (c, yo, yot)

        def stB3(ci):
            c, yo, yot = ctx.pop(("b2", ci))
            sT, sTt = str_.next()
            for q4 in range(4):
                tb = itc[0] % 2
                itc[0] += 1
                bank = 2 + tb
                pst = PSB[:, bank * 1024:bank * 1024 + 512]
                for j in range(4):
                    kk = q4 * 4 + j
                    p.op("pe", lambda e: e.transpose(pst[:, j * 128:(j + 1) * 128], yo[:, kk * 128:(kk + 1) * 128], k.ident),
                         reads=[yot, k.tconst], writes=[ttr[tb]])
                p.op("act", lambda e: e.activation(sT[:, q4 * 4:(q4 + 1) * 4, :].rearrange("p a b -> p (a b)"), pst, AF.Copy),
                     reads=[ttr[tb]], writes=[sTt])
            for q_ in range(2):
                p.dma("sp", ssdT.rearrange("(kk p) t -> p kk t", p=128)[:, q_ * 8:(q_ + 1) * 8, c * 128:(c + 1) * 128], sT[:, q_ * 8:(q_ + 1) * 8, :], reads=[sTt])

        pipelined(NB, [stA, stB1, stB2, stB3])
        p.barrier()

    def phase_D(l):
        W = L[l]
        ar.reset(base_off)
        NT = 256
        wa = ar.carve([128, 16, 1024], BF16)
        ws = ar.carve([128, 16, 1024], BF16)
        wo = ar.carve([128, 8, 1024], BF16)
        tw = Tk()
        twa, tws, two = Tk(), Tk(), Tk()
        wa_src = W["wa"].rearrange("(h d) n -> d h n", d=64)
        for h4 in range(4):
            p.dma("pool", wa[0:64, h4 * 4:(h4 + 1) * 4, :], wa_src[:, h4 * 4:(h4 + 1) * 4, :], writes=[twa])
        ws_src = W["ws"].rearrange("(kk p) n -> p kk n", p=128)
        for h4 in range(4):
            p.dma("pool", ws[:, h4 * 4:(h4 + 1) * 4, :], ws_src[:, h4 * 4:(h4 + 1) * 4, :], writes=[tws])
        wo_src = W["wo"].rearrange("(kk p) n -> p kk n", p=128)
        for h4 in range(2):
            p.dma("pool", wo[:, h4 * 4:(h4 + 1) * 4, :], wo_src[:, h4 * 4:(h4 + 1) * 4, :], writes=[two])
        atr = Ring(ar, 2, [128, 16, NT], BF16)
        sr = Ring(ar, 2, [128, 16, NT], BF16)
        gtr = Ring(ar, 2, [128, 16, NT], BF16)
        xtr = Ring(ar, 2, [128, 8, NT], F32)
        mgr = Ring(ar, 2, [128, 8, NT], BF16)
        t1r = Ring(ar, 2, [128, NT], F32)
        t2r = Ring(ar, 2, [128, NT], F32)
        ssd_v = ssdT.rearrange("(kk p) t -> p kk t", p=128)
        g_v = gT.rearrange("(kk p) t -> p kk t", p=128)
        tP = k.tps
        n = 0
        for t in range(T // NT):
            t0 = t * NT
            at, att = atr.next()
            p.dma("sp", at[0:64], attnT[:, :, t0:t0 + NT], writes=[att])
            st, stt = sr.next()
            p.dma("sp", st, ssd_v[:, :, t0:t0 + NT], writes=[stt])
            gt, gtt = gtr.next()
            p.dma("sp", gt, g_v[:, :, t0:t0 + NT], writes=[gtt])
            xt, xtt = xtr.next()
            p.dma("sp", xt, xT_v[:, :, t0:t0 + NT], writes=[xtt])
            mg, mgt = mgr.next()
            for oc in range(8):
                ba, bs = (n % 2), 2 + (n % 2)
                n += 1
                for h in range(16):
                    p.op("pe", lambda e, ba=ba, h=h, oc=oc, at=at: e.matmul(PS[ba][:, 0:NT], wa[0:64, h, oc * 128:(oc + 1) * 128], at[0:64, h, :], start=(h == 0), stop=(h == 15)),
                         reads=[twa, att], writes=[tP[ba]])
                for kk in range(16):
                    p.op("pe", lambda e, bs=bs, kk=kk, oc=oc, st=st: e.matmul(PS[bs][:, 0:NT], ws[:, kk, oc * 128:(oc + 1) * 128], st[:, kk, :], start=(kk == 0), stop=(kk == 15)),
                         reads=[tws, stt], writes=[tP[bs]])
                t1, t1t = t1r.next()
                t2, t2t = t2r.next()
                p.op("dve", lambda e, t1=t1, ba=ba, gt=gt, oc=oc: e.tensor_tensor(t1, PS[ba][:, 0:NT], gt[:, oc, :], ALU.mult), reads=[tP[ba], gtt], writes=[t1t])
                p.op("dve", lambda e, t2=t2, bs=bs, gt=gt, oc=oc: e.tensor_tensor(t2, PS[bs][:, 0:NT], gt[:, 8 + oc, :], ALU.mult), reads=[tP[bs], gtt], writes=[t2t])
                p.op("pool", lambda e, mg=mg, oc=oc, t1=t1, t2=t2: e.tensor_tensor(mg[:, oc, :], t1, t2, ALU.add), reads=[t1t, t2t], writes=[mgt])
            for oc in range(8):
                bo = 4 + (oc % 2)
                for kk in range(8):
                    p.op("pe", lambda e, bo=bo, kk=kk, oc=oc, mg=mg: e.matmul(PS[bo][:, 0:NT], wo[:, kk, oc * 128:(oc + 1) * 128], mg[:, kk, :], start=(kk == 0), stop=(kk == 7)),
                         reads=[two, mgt], writes=[tP[bo]])
                p.op("dve", lambda e, bo=bo, xt=xt, oc=oc: e.tensor_tensor(xt[:, oc, :], PS[bo][:, 0:NT], xt[:, oc, :], ALU.add), reads=[tP[bo], xtt], writes=[xtt])
            p.dma("sp", xT_v[:, :, t0:t0 + NT], xt, reads=[xtt])
        p.barrier()

    def phase_E(l):
        W = L[l]
        ar.reset(base_off)
        NT = 512
        g_x = k.gn[:, (4 * l + 1) * 8:(4 * l + 1) * 8 + 8]
        g_m = k.gn[:, (4 * l + 2) * 8:(4 * l + 2) * 8 + 8]
        mt = ar.carve([128, 8, 256], F32)
        memn = ar.carve([128, 8, 256], BF16)
        wk = ar.carve([128, 8, 512], BF16)
        wv = ar.carve([128, 8, 512], BF16)
        wq = ar.carve([128, 8, 512], BF16)
        wxo = ar.carve([128, 4, 1024], BF16)
        kxT = ar.carve([128, 4, 256], BF16)
        vx = ar.carve([128, 2, 512], BF16)
        sq = ar.carve([128, 8, NT], F32)
        rst = ar.carve([128, NT], F32)
        tw, tmt_, tmn, tsq, trs, tkx, tvx = [Tk() for _ in range(7)]
        xtr = Ring(ar, 2, [128, 8, NT], F32)
        htr = Ring(ar, 1, [128, 8, NT], BF16)
        qxr = Ring(ar, 1, [128, 4, NT], BF16)
        ptr = Ring(ar, 2, [128, 2, NT], BF16)
        rcr = Ring(ar, 2, [128, NT], F32)
        oxr = Ring(ar, 1, [128, 4, NT], BF16)
        tP = k.tps
        kv_src = W["wxkv"].rearrange("(c p) n -> p c n", p=128)
        p.dma("sp", mt, memT_in.rearrange("(c p) m -> p c m", p=128), writes=[tmt_])
        p.dma("pool", wk, kv_src[:, :, 0:512], writes=[tw])
        p.dma("pool", wv, kv_src[:, :, 512:1024], writes=[tw])
        p.dma("pool", wq, W["wxq"].rearrange("(c p) n -> p c n", p=128), writes=[tw])
        p.dma("pool", wxo, W["wxo"].rearrange("(h p) n -> p h n", p=128), writes=[tw])
        rmsnorm_tile(k, mt, tmt_, 256, g_m, lambda c: (memn[:, c, :], [tmn]), sq, tsq, rst, trs, PS[0], tP[0])
        for h in range(4):
            b = 1 + (h % 2)
            for c in range(8):
                p.op("pe", lambda e, b=b, c=c, h=h: e.matmul(PS[b][:, 0:256], wk[:, c, h * 128:(h + 1) * 128], memn[:, c, :], start=(c == 0), stop=(c == 7)),
                     reads=[tw, tmn], writes=[tP[b]])
            p.op("act", lambda e, b=b, h=h: e.activation(kxT[:, h, :], PS[b][:, 0:256], AF.Copy), reads=[tP[b]], writes=[tkx])
        for mb in range(2):
            b = 3 + mb
            for c in range(8):
                p.op("pe", lambda e, b=b, c=c, mb=mb: e.matmul(PS[b], memn[:, c, mb * 128:(mb + 1) * 128], wv[:, c, :], start=(c == 0), stop=(c == 7)),
                     reads=[tw, tmn], writes=[tP[b]])
            p.op("dve", lambda e, b=b, mb=mb: e.tensor_copy(vx[:, mb, :], PS[b]), reads=[tP[b]], writes=[tvx])
        n = 0
        for t in range(T // NT):
            t0 = t * NT
            xt, xtt = xtr.next()
            p.dma("sp", xt, xT_v[:, :, t0:t0 + NT], writes=[xtt])
            ht, htt = htr.next()
            rmsnorm_tile(k, xt, xtt, NT, g_x, lambda c, ht=ht, htt=htt: (ht[:, c, :], [htt]), sq, tsq, rst, trs, PS[0], tP[0])
            qx, qxt = qxr.next()
            for h in range(4):
                b = 1 + (h % 2)
                for c in range(8):
                    p.op("pe", lambda e, b=b, c=c, h=h, ht=ht: e.matmul(PS[b], wq[:, c, h * 128:(h + 1) * 128], ht[:, c, :], start=(c == 0), stop=(c == 7)),
                         reads=[tw, htt], writes=[tP[b]])
                if h % 2 == 0:
                    p.op("act", lambda e, b=b, h=h, qx=qx: e.activation(qx[:, h, :], PS[b], AF.Copy), reads=[tP[b]], writes=[qxt])
                else:
                    p.op("dve", lambda e, b=b, h=h, qx=qx: e.tensor_copy(qx[:, h, :], PS[b]), reads=[tP[b]], writes=[qxt])
            ox, oxt = oxr.next()
            for h in range(4):
                pt_, ptt = ptr.next()
                for mb in range(2):
                    b = 3 + mb
                    p.op("pe", lambda e, b=b, h=h, mb=mb, qx=qx: e.matmul(PS[b], kxT[:, h, mb * 128:(mb + 1) * 128], qx[:, h, :], start=True, stop=True),
                         reads=[tkx, qxt], writes=[tP[b]])
                    p.op("act", lambda e, b=b, mb=mb, pt_=pt_: e.activation(pt_[:, mb, :], PS[b], AF.Exp, scale=float(128 ** -0.5)), reads=[tP[b]], writes=[ptt])
                for mb in range(2):
                    p.op("pe", lambda e, h=h, mb=mb, pt_=pt_: e.matmul(PS[5], vx[:, mb, h * 128:(h + 1) * 128], pt_[:, mb, :], start=(mb == 0), stop=(mb == 1)),
                         reads=[tvx, ptt], writes=[tP[5]])
                for mb in range(2):
                    p.op("pe", lambda e, mb=mb, pt_=pt_: e.matmul(PS[6], k.onesb, pt_[:, mb, :], start=(mb == 0), stop=(mb == 1)),
                         reads=[k.tconst, ptt], writes=[tP[6]])
                rc, rct = rcr.next()
                p.op("act", lambda e, rc=rc: e.activation(rc, PS[6], AF.Ln), reads=[tP[6]], writes=[rct])
                p.op("act", lambda e, rc=rc: e.activation(rc, rc, AF.Exp, scale=-1.0), reads=[rct], writes=[rct])
                p.op("dve", lambda e, rc=rc, ox=ox, h=h: e.tensor_tensor(ox[:, h, :], PS[5], rc, ALU.mult), reads=[tP[5], rct], writes=[oxt])
            for oc in range(8):
                b = 7 if oc % 2 == 0 else 1
                for h in range(4):
                    p.op("pe", lambda e, b=b, h=h, oc=oc, ox=ox: e.matmul(PS[b], wxo[:, h, oc * 128:(oc + 1) * 128], ox[:, h, :], start=(h == 0), stop=(h == 3)),
                         reads=[tw, oxt], writes=[tP[b]])
                p.op("dve", lambda e, b=b, xt=xt, oc=oc: e.tensor_tensor(xt[:, oc, :], PS[b], xt[:, oc, :], ALU.add), reads=[tP[b], xtt], writes=[xtt])
            p.dma("sp", xT_v[:, :, t0:t0 + NT], xt, reads=[xtt])
        p.barrier()

    def phase_F(l, last):
        W = L[l]
        ar.reset(base_off)
        NT = 256
        g_f = k.gn[:, (4 * l + 3) * 8:(4 * l + 3) * 8 + 8]
        g_o = k.gn[:, 64:72]
        wup = ar.carve([128, 8, 4096], BF16)
        wdn = ar.carve([128, 32, 1024], BF16)
        tw = Tk()
        up_src = W["wup"].rearrange("(c p) n -> p c n", p=128)
        dn_src = W["wdn"].rearrange("(c p) n -> p c n", p=128)
        twu = [Tk() for _ in range(8)]
        twd = [Tk() for _ in range(8)]
        for i_ in range(8):
            p.dma("pool", wup[:, :, i_ * 512:(i_ + 1) * 512], up_src[:, :, i_ * 512:(i_ + 1) * 512], writes=[twu[i_]])
        for i_ in range(8):
            p.dma("pool", wdn[:, i_ * 4:(i_ + 1) * 4, :], dn_src[:, i_ * 4:(i_ + 1) * 4, :], writes=[twd[i_]])
        sq = ar.carve([128, 8, NT], F32)
        rst = ar.carve([128, NT], F32)
        tsq, trs = Tk(), Tk()
        xtr = Ring(ar, 2, [128, 8, NT], F32)
        htr = Ring(ar, 2, [128, 8, NT], BF16)
        acr = Ring(ar, 1, [128, 32, NT], BF16)
        rlr = Ring(ar, 2, [128, NT], F32)
        fo = ar.carve([128, 8, NT], F32)
        tfo = Tk()
        tP = k.tps
        out_v = outT.rearrange("(c p) t -> p c t", p=128)
        ctx = {}

        def stN(t):
            t0 = t * NT
            xt, xtt = xtr.next()
            p.dma("sp", xt, xT_v[:, :, t0:t0 + NT], writes=[xtt])
            ht, htt = htr.next()
            rmsnorm_tile(k, xt, xtt, NT, g_f, lambda c, ht=ht, htt=htt: (ht[:, c, :], [htt]), sq, tsq, rst, trs, PS[0], tP[0])
            ctx[t] = (xt, xtt, ht, htt)

        def stU(t):
            xt, xtt, ht, htt = ctx[t]
            ac, act_ = acr.next()
            for fc in range(32):
                b = 1 + (fc % 3)
                for c in range(8):
                    p.op("pe", lambda e: e.matmul(PS[b][:, 0:NT], wup[:, c, fc * 128:(fc + 1) * 128], ht[:, c, :], start=(c == 0), stop=(c == 7)),
                         reads=[twu[fc // 4], htt], writes=[tP[b]])
                rl, rlt = rlr.next()
                p.op("act", lambda e: e.activation(rl, PS[b][:, 0:NT], AF.Relu), reads=[tP[b]], writes=[rlt])
                p.op("pool", lambda e: e.tensor_tensor(ac[:, fc, :], rl, rl, ALU.mult), reads=[rlt], writes=[act_])
            ctx[t] = (xt, xtt, ac, act_)

        def stD(t):
            t0 = t * NT
            xt, xtt, ac, act_ = ctx.pop(t)
            for oc in range(8):
                b = 4 + (oc % 3)
                for fc in range(32):
                    p.op("pe", lambda e: e.matmul(PS[b][:, 0:NT], wdn[:, fc, oc * 128:(oc + 1) * 128], ac[:, fc, :], start=(fc == 0), stop=(fc == 31)),
                         reads=[twd[fc // 4], act_], writes=[tP[b]])
                p.op("dve", lambda e: e.tensor_tensor(xt[:, oc, :], PS[b][:, 0:NT], xt[:, oc, :], ALU.add), reads=[tP[b], xtt], writes=[xtt])
            if last:
                rmsnorm_tile(k, xt, xtt, NT, g_o, lambda c: (fo[:, c, :], [tfo]), sq, tsq, rst, trs, PS[7], tP[7])
                p.dma("sp", out_v[:, :, t0:t0 + NT], fo, reads=[tfo])
            else:
                p.dma("sp", xT_v[:, :, t0:t0 + NT], xt, reads=[xtt])

        ntile = T // NT
        stN(0)
        for t in range(ntile):
            stU(t)
            if t + 1 < ntile:
                stN(t + 1)
            stD(t)
        p.barrier()

    def phase_X2():
        ar.reset(base_off)
        xs = ar.carve([128, 8, 128], F32)
        x1 = ar.carve([128, 128], F32)
        xr = ar.carve([128, 8, 128], F32)
        G = ar.carve([128, 2, 8, 128], F32)
        xh = ar.carve([128, 8, 128], F32)
        txs, tx1, txr, tG, txh, tcc = [Tk() for _ in range(6)]
        tP = k.tps
        p.dma("sp", xs, xT_v[:, :, T - 128:T], writes=[txs])
        for c in range(8):
            p.op("pe", lambda e, c=c: e.matmul(PS[0][:, 0:128], xs[:, c, :], k.ident32, start=True, stop=True), reads=[txs, k.tconst], writes=[tP[0]])
            p.op("act", lambda e: e.activation(x1, PS[0][:, 0:128], AF.Copy), reads=[tP[0]], writes=[tx1])
            p.op("pe", lambda e: e.matmul(PS[1][:, 0:128], x1, k.J32, start=True, stop=True), reads=[tx1, k.tconst], writes=[tP[1]])
            p.op("dve", lambda e, c=c: e.tensor_copy(xr[:, c, :], PS[1][:, 0:128]), reads=[tP[1]], writes=[txr])
        p.dma("sp", ccX_in.rearrange("(c p) t -> p c t", p=128), xr, reads=[txr])
        p.barrier()
        p.cc(lambda e: e.collective_compute("AllGather", ALU.bypass, replica_groups=pairs, ins=[ccX_in.opt()], outs=[ccX_out.opt()]), writes=[tcc])
        p.dma("sp", G[:, 0], ccX_out[0:D, :].rearrange("(c p) t -> p c t", p=128), reads=[tcc], writes=[tG])
        p.dma("sp", G[:, 1], ccX_out[D:2 * D, :].rearrange("(c p) t -> p c t", p=128), reads=[tcc], writes=[tG])
        xh2 = xh.rearrange("p a b -> p (a b)")
        p.op("dve", lambda e: e.tensor_scalar(xh2, G[:, 0].rearrange("p a b -> p (a b)"), k.sel[:, 0:1], None, ALU.mult), reads=[tG, k.tconst], writes=[txh])
        p.op("dve", lambda e: e.scalar_tensor_tensor(xh2, G[:, 1].rearrange("p a b -> p (a b)"), k.sel[:, 1:2], xh2, ALU.mult, ALU.add), reads=[tG, k.tconst, txh], writes=[txh])
        p.dma("sp", xT_v[:, :, T:TE], xh, reads=[txh])
        p.barrier()

    for l in range(nlayers):
        phase_A(l)
        if stop_after == "A":
            break
        phase_B(l)
        if stop_after == "B":
            break
        phase_C1(l)
        if stop_after == "C1":
            break
        phase_C2(l)
        if stop_after == "C2":
            break
        phase_D(l)
        if stop_after == "D":
            break
        phase_E(l)
        if stop_after == "E":
            break
        phase_F(l, last=(l == nlayers - 1))
        if stop_after == "F":
            break
        if l < nlayers - 1:
            phase_X2()

    p.barrier()
    p.emit()
    LAST_PROG[0] = p
    return nc


def host_consts():
    kk = np.arange(128)[:, None]
    ll = np.arange(128)[None, :]
    ident = (kk == ll).astype(np.float32)
    triU = (kk <= ll).astype(np.float32)
    triL = (kk >= ll).astype(np.float32)
    maskA = np.where(ll >= kk, 0.0, NEG).astype(np.float32)
    maskB = np.where(kk >= ll, 0.0, NEG).astype(np.float32)
    ones = np.ones((128, 128), np.float32)
    Jm = (kk + ll == 127).astype(np.float32)
    cst = np.concatenate([ident, triU, triL, -triU, -triL, maskA, maskB, ones, Jm], axis=1)
    slopes = np.array([2.0 ** (-8.0 * (h + 1) / 16) for h in range(16)], np.float32)
    s = np.arange(128)[:, None, None, None]
    jj = np.arange(3)[None, :, None, None]
    q = np.arange(128)[None, None, None, :]
    rel = np.abs(q - (s + (jj - 1) * 128)).astype(np.float32)
    bias8 = np.where(rel <= 128, -8.0 * slopes[None, None, :, None] * rel, 8.0 * NEG).astype(np.float32)
    hi = bias8.astype(ml_dtypes.bfloat16).astype(np.float32)
    lo = (bias8 - hi).astype(ml_dtypes.bfloat16).astype(np.float32)
    em = np.concatenate([hi.reshape(128, 6144), lo.reshape(128, 6144)], axis=1)
    return cst, em


def gvec(g):
    return np.ascontiguousarray(np.asarray(g, np.float32).reshape(8, 128).T)


def prep_inputs(inputs, ncores=8, nlayers=2):
    I = {k_: np.asarray(v) for k_, v in inputs.items()}
    cst, em = host_consts()
    common = {"cst": cst, "em": em}
    per_par = [dict(), dict()]
    OFF_K, OFF_V, OFF_Z, OFF_XBC, OFF_DT, OFF_G = 1024, 1280, 1536, 3584, 7680, 7744
    for l in range(nlayers):
        w = I["w_in"][l]
        common["wfm%d" % l] = np.ascontiguousarray(np.concatenate([w[:, 0:1024], w[:, OFF_K:OFF_V], w[:, OFF_G:OFF_G + 2048], w[:, OFF_XBC:OFF_DT]], axis=1))
        common["wtm%d" % l] = np.ascontiguousarray(np.concatenate([w[:, OFF_V:OFF_Z], w[:, OFF_Z:OFF_XBC]], axis=1))
        for par in range(2):
            dd = per_par[par]
            order = [0, 1] if par == 0 else [1, 0]
            wdt = w[:, OFF_DT:OFF_G].reshape(1024, 2, 32)[:, order, :].reshape(1024, 64)
            dd["wdt%d" % l] = np.ascontiguousarray(wdt)
            cw = I["conv_w"][l]
            if par == 1:
                cw = cw[::-1]
            dd["cw%d" % l] = np.ascontiguousarray(cw.reshape(5, 32, 128).transpose(2, 1, 0).reshape(128, 160))
            dd["dtb%d" % l] = np.ascontiguousarray(I["dt_bias"][l][order].reshape(1, 64))
            dd["alog%d" % l] = np.ascontiguousarray(I["a_log"][l][order].reshape(1, 64))
        common["cb%d" % l] = np.ascontiguousarray(I["conv_b"][l].reshape(32, 128).T)
        common["cbrow%d" % l] = np.ascontiguousarray(I["conv_b"][l].reshape(1, 4096))
        common["dsk%d" % l] = np.ascontiguousarray(I["d_skip"][l].reshape(1, 32))
        common["sng%d" % l] = np.ascontiguousarray(I["ssd_norm"][l].reshape(1, 2048))
        common["sink%d" % l] = np.ascontiguousarray(I["attn_sink"][l].reshape(1, 16))
        common["wa%d" % l] = I["w_attn_branch"][l]
        common["ws%d" % l] = I["w_ssd_branch"][l]
        common["wo%d" % l] = I["w_out"][l]
        common["wxq%d" % l] = I["w_xq"][l]
        common["wxkv%d" % l] = I["w_xkv"][l]
        common["wxo%d" % l] = I["w_xo"][l]
        common["wup%d" % l] = I["w_up"][l]
        common["wdn%d" % l] = I["w_down"][l]
    gl = []
    for l in range(2):
        ll_ = min(l, nlayers - 1)
        gl += [gvec(I["norm_mix"][ll_]), gvec(I["norm_cross"][ll_]), gvec(I["norm_mem"][ll_]), gvec(I["norm_ffn"][ll_])]
    gl.append(gvec(I["norm_final"]))
    common["gn"] = np.ascontiguousarray(np.concatenate(gl, axis=1))
    in_maps = []
    for c in range(ncores):
        b, par = c // 2, c % 2
        xs = I["x"][b]
        if par == 1:
            xs = xs[::-1]
        m = dict(common)
        m.update(per_par[par])
        m["xT"] = np.ascontiguousarray(xs[:TE].T)
        m["memT"] = np.ascontiguousarray(I["mem"][b].T)
        m["sel"] = np.tile(np.array([[0.0, 1.0]] if par == 0 else [[1.0, 0.0]], np.float32), (128, 1))
        in_maps.append(m)
    return in_maps


_NC_CACHE = {}


def kernel(**inputs):
    ncores = 8
    in_maps = prep_inputs(inputs, ncores)
    if "nc" not in _NC_CACHE:
        _NC_CACHE["nc"] = build(ncores)
    nc = _NC_CACHE["nc"]
    res = run_bass_kernel_spmd(nc, in_maps, core_ids=list(range(ncores)))
    out = np.empty((4, 8192, D), np.float32)
    for c in range(ncores):
        b, par = c // 2, c % 2
        o = res.results[c]["outT"].T
        if par == 0:
            out[b, 0:T] = o
        else:
            out[b, T:] = o[::-1]
    return out
```
